# Optimizing a Trainium2 kernel written in Bass

```python
import math
import jax, jax.numpy as jnp
from jax import lax
import numpy as np

D_MODEL = 2048
BATCH = 4
SEQ = 2048
DEPTH = 1

MEM_LEN = 256
DIFF_HEADS = 8
DIFF_HEAD_DIM = 128
DIFF_QK_WIDTH = DIFF_HEADS * 2 * DIFF_HEAD_DIM
DIFF_V_WIDTH = DIFF_HEADS * 2 * DIFF_HEAD_DIM
Q_BLOCK = 128
CONV_WIDTH = D_MODEL
CONV_K = 3
MIX_IN_SPLITS = (DIFF_QK_WIDTH, DIFF_QK_WIDTH, DIFF_V_WIDTH,
                 CONV_WIDTH, CONV_WIDTH, CONV_WIDTH,
                 D_MODEL, D_MODEL)
MIX_IN_WIDTH = sum(MIX_IN_SPLITS)
MIX_IN_OFFSETS = tuple(int(o) for o in np.cumsum(MIX_IN_SPLITS)[:-1])
N_BRANCHES = 2
XATTN_HEADS = 4
XATTN_HEAD_DIM = 128
XATTN_WIDTH = XATTN_HEADS * XATTN_HEAD_DIM
D_FF = 128 * ((8 * D_MODEL // 3 + 127) // 128)
NORM_EPS = 1e-6
SUBLN_EPS = 1e-5

kernel_name = 'hybrid_diffattn_shortconv_macaron'


def rms_norm(x, g, eps=NORM_EPS):
    xf = x.astype(jnp.float32)
    y = xf * lax.rsqrt(jnp.mean(xf * xf, axis=-1, keepdims=True) + eps)
    return (y * g.astype(jnp.float32)).astype(x.dtype)


def swiglu(h, w_gate, w_up, w_down):
    return (jax.nn.silu(h @ w_gate) * (h @ w_up)) @ w_down


def diff_lambda(lq1, lk1, lq2, lk2, lam_init):
    f = jnp.float32
    return (jnp.exp(jnp.dot(lq1.astype(f), lk1.astype(f)))
            - jnp.exp(jnp.dot(lq2.astype(f), lk2.astype(f))) + lam_init)


def diff_attention(q, k, v, lam):
    s = q.shape[1]
    scale = q.shape[-1] ** -0.5
    qh = jnp.transpose(q, (0, 2, 3, 1, 4)) * scale
    kh = jnp.transpose(k, (0, 2, 3, 1, 4))
    vh = jnp.transpose(v, (0, 2, 1, 3))
    outs = []
    for i in range(s // Q_BLOCK):
        q0, q1 = i * Q_BLOCK, (i + 1) * Q_BLOCK
        qb = qh[:, :, :, q0:q1]
        kb = kh[:, :, :, :q1]
        vb = vh[:, :, :q1]
        sc = jnp.einsum('bhcqd,bhckd->bhcqk', qb, kb).astype(jnp.float32)
        causal = jnp.arange(q1)[None, :] <= jnp.arange(q0, q1)[:, None]
        sc = jnp.where(causal, sc, -jnp.inf)
        p = jax.nn.softmax(sc, axis=-1)
        a = p[:, :, 0] - lam * p[:, :, 1]
        outs.append(jnp.einsum('bhqk,bhkd->bhqd', a.astype(v.dtype), vb))
    o = jnp.concatenate(outs, axis=2)
    return jnp.transpose(o, (0, 2, 1, 3))


def causal_depthwise_conv(u, w):
    c = u.shape[-1]
    return lax.conv_general_dilated(
        u, w[:, None, :], window_strides=(1,), padding=((CONV_K - 1, 0),),
        dimension_numbers=('NWC', 'WIO', 'NWC'), feature_group_count=c)


def cross_attention(h, m, w_q, w_kv, w_o):
    b, s, _ = h.shape
    n_mem = m.shape[1]
    q = (h @ w_q).reshape(b, s, XATTN_HEADS, XATTN_HEAD_DIM)
    kv = (m @ w_kv).reshape(b, n_mem, 2, XATTN_HEADS, XATTN_HEAD_DIM)
    k, v = kv[:, :, 0], kv[:, :, 1]
    sc = jnp.einsum('bshd,bmhd->bhsm', q, k).astype(jnp.float32) * (XATTN_HEAD_DIM ** -0.5)
    p = jax.nn.softmax(sc, axis=-1)
    o = jnp.einsum('bhsm,bmhd->bshd', p.astype(v.dtype), v).reshape(b, s, XATTN_WIDTH)
    return o @ w_o


def setup_inputs(seed: int = 0) -> dict:
    key = jax.random.key(seed)
    ks = jax.random.split(key, 32)
    f = jnp.float32

    def w(k, shape, fan_in):
        return jax.random.normal(k, shape, f) * (fan_in ** -0.5)

    def g(k, shape):
        return 1.0 + 0.01 * jax.random.normal(k, shape, f)

    L, D = DEPTH, D_MODEL
    return {
        'x': jax.random.normal(ks[0], (BATCH, SEQ, D), f),
        'mem': jax.random.normal(ks[1], (BATCH, MEM_LEN, D), f),
        'ffn1_norm': g(ks[2], (L, D)),
        'ffn1_w_gate': w(ks[3], (L, D, D_FF), D),
        'ffn1_w_up': w(ks[4], (L, D, D_FF), D),
        'ffn1_w_down': w(ks[5], (L, D_FF, D), D_FF),
        'mix_norm': g(ks[6], (L, D)),
        'w_mix_in': w(ks[7], (L, D, MIX_IN_WIDTH), D),
        'b_gates': 0.1 * jax.random.normal(ks[8], (L, N_BRANCHES, D), f),
        'lambda_q1': 0.1 * jax.random.normal(ks[9], (L, DIFF_HEAD_DIM), f),
        'lambda_k1': 0.1 * jax.random.normal(ks[10], (L, DIFF_HEAD_DIM), f),
        'lambda_q2': 0.1 * jax.random.normal(ks[11], (L, DIFF_HEAD_DIM), f),
        'lambda_k2': 0.1 * jax.random.normal(ks[12], (L, DIFF_HEAD_DIM), f),
        'diff_subln': g(ks[13], (L, 2 * DIFF_HEAD_DIM)),
        'w_attn_out': w(ks[14], (L, DIFF_V_WIDTH, D), DIFF_V_WIDTH),
        'conv_w': w(ks[15], (L, CONV_K, CONV_WIDTH), CONV_K),
        'w_conv_out': w(ks[16], (L, CONV_WIDTH, D), CONV_WIDTH),
        'w_mix_out': w(ks[17], (L, D, D), D),
        'xattn_norm': g(ks[18], (L, D)),
        'mem_norm': g(ks[19], (L, D)),
        'w_xq': w(ks[20], (L, D, XATTN_WIDTH), D),
        'w_xkv': w(ks[21], (L, D, 2 * XATTN_WIDTH), D),
        'w_xo': w(ks[22], (L, XATTN_WIDTH, D), XATTN_WIDTH),
        'ffn2_norm': g(ks[23], (L, D)),
        'ffn2_w_gate': w(ks[24], (L, D, D_FF), D),
        'ffn2_w_up': w(ks[25], (L, D, D_FF), D),
        'ffn2_w_down': w(ks[26], (L, D_FF, D), D_FF),
        'final_norm': g(ks[27], (D,)),
    }


def reference(x, mem, ffn1_norm, ffn1_w_gate, ffn1_w_up, ffn1_w_down,
              mix_norm, w_mix_in, b_gates, lambda_q1, lambda_k1, lambda_q2, lambda_k2,
              diff_subln, w_attn_out, conv_w, w_conv_out, w_mix_out,
              xattn_norm, mem_norm, w_xq, w_xkv, w_xo,
              ffn2_norm, ffn2_w_gate, ffn2_w_up, ffn2_w_down, final_norm):
    b, s, _ = x.shape
    for l in range(DEPTH):
        x = x + 0.5 * swiglu(rms_norm(x, ffn1_norm[l]), ffn1_w_gate[l], ffn1_w_up[l], ffn1_w_down[l])

        h = rms_norm(x, mix_norm[l])
        z = h @ w_mix_in[l]
        q, k, v, gate_b, gate_c, u, ga_pre, gc_pre = jnp.split(z, MIX_IN_OFFSETS, axis=-1)

        lam_init = 0.8 - 0.6 * math.exp(-0.3 * l)
        lam = diff_lambda(lambda_q1[l], lambda_k1[l], lambda_q2[l], lambda_k2[l], lam_init)
        ya = diff_attention(q.reshape(b, s, DIFF_HEADS, 2, DIFF_HEAD_DIM),
                            k.reshape(b, s, DIFF_HEADS, 2, DIFF_HEAD_DIM),
                            v.reshape(b, s, DIFF_HEADS, 2 * DIFF_HEAD_DIM), lam)
        ya = rms_norm(ya, diff_subln[l], SUBLN_EPS) * (1.0 - lam_init)
        ya = ya.reshape(b, s, DIFF_V_WIDTH) @ w_attn_out[l]

        yc = (gate_b * causal_depthwise_conv(gate_c * u, conv_w[l])) @ w_conv_out[l]

        ga = jax.nn.sigmoid(ga_pre + b_gates[l, 0])
        gc = jax.nn.sigmoid(gc_pre + b_gates[l, 1])
        x = x + (ga * ya + gc * yc) @ w_mix_out[l]

        x = x + cross_attention(rms_norm(x, xattn_norm[l]), rms_norm(mem, mem_norm[l]),
                                w_xq[l], w_xkv[l], w_xo[l])

        x = x + 0.5 * swiglu(rms_norm(x, ffn2_norm[l]), ffn2_w_gate[l], ffn2_w_up[l], ffn2_w_down[l])
    return rms_norm(x, final_norm)
```

```python
import numpy as np
import os as _os
from contextlib import ExitStack
import concourse.bass as bass
import concourse.mybir as mybir
from concourse.bass_utils import run_bass_kernel_spmd

F32 = mybir.dt.float32
BF16 = mybir.dt.bfloat16
AF = mybir.ActivationFunctionType
ALU = mybir.AluOpType

D = 2048
DFF = 5504
NJ = 43
GJ = 11
NG = 4
TOK = 1024
T = 512
NS = 4
SLOT = 4096
SCALE = 128 ** -0.5

P_FFN1, P_MIX, P_XA, P_MEM, P_FFN2, P_FIN = 0, 16, 32, 48, 64, 80
P_BGA, P_BGC = 96, 112
P_CW = 128
P_SUB = 176
P_FLAG = 178
P_LAM = 180
NP = P_LAM + 512


def plan():
    keys = []
    for f in (1, 2):
        for j in range(NJ):
            keys.append((('gu', f, j), 4096))
        for g in range(NG):
            nj = min(GJ, NJ - GJ * g)
            for c in range(16):
                keys.append((('dn', f, g, c), nj * 128))
    for hd in range(8):
        for nm in 'qkv':
            keys.append(((nm, hd), 4096))
    for c in range(16):
        for nm in ('cB', 'cC', 'cU', 'ga', 'gc', 'ao', 'co', 'mo'):
            keys.append(((nm, c), 2048))
    for hd in range(4):
        keys.append((('xq', hd), 2048))
        keys.append((('xk', hd), 2048))
    for i in range(2):
        keys.append((('xv', i), 4096))
        keys.append((('xo', i), 4096))
    off = {}
    o = 0
    for k, s in keys:
        off[k] = (o, s)
        o += s
    return off, o


def _is_ap(x):
    return hasattr(x, 'offset') and hasattr(x, 'space') and hasattr(x, 'ap')


class Sched:
    G = 256

    def __init__(self):
        self.ops = {e: [] for e in ('pe', 'act', 'dve', 'pool', 'sp')}
        self.cnt = {}
        self.last = {}
        self.W = {}
        self.R = {}

    def _cells(self, ap):
        if 'DRAM' in str(ap.space):
            return ()
        dsz = mybir.dt.size(ap.dtype)
        dims = [(st, n) for st, n in list(ap.ap)[1:] if n > 1]
        nm = ap.tensor.name
        run = 1
        while dims and dims[-1][0] == run:
            run *= dims[-1][1]
            dims.pop()
        outer = 1
        for _, n in dims:
            outer *= n
        starts = [ap.offset]
        if outer <= 256:
            for st, n in dims:
                starts = [b + i * st for b in starts for i in range(n)]
            ext = run
        else:
            ext = run
            for st, n in dims:
                ext += (n - 1) * abs(st)
        cells = set()
        for b in starts:
            lo = b * dsz
            hi = lo + ext * dsz
            cells.update(range(lo // self.G, (hi - 1) // self.G + 1))
        return [(nm, c) for c in sorted(cells)]

    def op(self, eng, fn, waits=(), sig=False, dma_sem=None, rd=(), wr=()):
        rds, wrs = list(rd), list(wr)
        if fn is not None and fn.__defaults__:
            first = True
            for dflt in fn.__defaults__:
                if _is_ap(dflt):
                    (wrs if first else rds).append(dflt)
                    first = False
                elif isinstance(dflt, dict):
                    rds.extend(v for v in dflt.values() if _is_ap(v))
        tok = None
        if dma_sem is not None:
            self.cnt[dma_sem] = self.cnt.get(dma_sem, 0) + 16
            tok = (dma_sem, self.cnt[dma_sem])
            mark = tok
        elif sig:
            self.cnt[eng] = self.cnt.get(eng, 0) + 1
            tok = (eng, self.cnt[eng])
            self.last[eng] = tok
            mark = tok
        else:
            mark = (eng, self.cnt.get(eng, 0) + 1)
            assert eng == 'pe' or fn is None, "non-signalling op on a non-PE engine"
        ws = []
        for w in waits:
            if w is None:
                continue
            if isinstance(w, list):
                ws.extend([x for x in w if x is not None])
            else:
                ws.append(w)
        auto = {}
        rcells = [c for a in rds for c in self._cells(a)]
        wcells = [c for a in wrs for c in self._cells(a)]
        for c in rcells:
            for k, v in self.W.get(c, {}).items():
                auto[k] = max(auto.get(k, 0), v)
        for c in wcells:
            for d in (self.W.get(c, {}), self.R.get(c, {})):
                for k, v in d.items():
                    auto[k] = max(auto.get(k, 0), v)
        for k, v in auto.items():
            if eng == 'pe' and k == 'pe':
                continue
            if _os.environ.get("K_NOSAME") and k == eng:
                continue
            if _os.environ.get("K_NOAUTO"):
                continue
            if k == 'pe':
                assert v <= self.cnt.get('pe', 0), "dependency on a PE signal that is not emitted yet"
            if mark is not None and k == mark[0] and v >= mark[1]:
                continue
            ws.append((k, v))
        for c in rcells:
            d = self.R.setdefault(c, {})
            d[mark[0]] = max(d.get(mark[0], 0), mark[1])
        for c in wcells:
            if eng == 'pe':
                d = self.W.setdefault(c, {})
                d[mark[0]] = max(d.get(mark[0], 0), mark[1])
                self.R[c] = {k: v for k, v in self.R.get(c, {}).items() if k == 'pe'}
            else:
                self.W[c] = {mark[0]: mark[1]}
                self.R[c] = {}
        self.ops[eng].append((fn, ws, tok, dma_sem is not None))
        return tok


def build():
    nc = bass.Bass("TRN2", target_bir_lowering=False)
    OFF, TOTAL = plan()
    xo_d = nc.dram_tensor("x_own", [16, 128, TOK], F32, kind="ExternalInput").ap()
    xp_d = nc.dram_tensor("x_prev", [16, 128, TOK], F32, kind="ExternalInput").ap()
    mem_d = nc.dram_tensor("memT", [16, 128, 256], F32, kind="ExternalInput").ap()
    prm_d = nc.dram_tensor("prm", [128, NP], F32, kind="ExternalInput").ap()
    cst_d = nc.dram_tensor("cst", [128, 384], F32, kind="ExternalInput").ap()
    w_d = nc.dram_tensor("wbig", [128, TOTAL], F32, kind="ExternalInput").ap()
    out_d = nc.dram_tensor("out", [16, 128, TOK], F32, kind="ExternalOutput").ap()
    xs_d = nc.dram_tensor("xspill", [16, 128, TOK], F32, kind="Internal").ap()

    S = Sched()
    with ExitStack() as es:
        A = es.enter_context(nc.sbuf_tensor("arena", [128, 103 * 1024], BF16))
        PS = [es.enter_context(nc.psum_tensor(f"ps{i}", [128, 512], F32))[:, :] for i in range(8)]
        sem_names = ['pe', 'act', 'dve', 'pool', 'w0', 'w1', 'w2', 'w3', 'ldx0', 'ldx1', 'ldx2', 'ldx3', 'ldx4', 'ldx5', 'ldx6', 'ldx7', 'ldp', 'ldc', 'ldm', 'st', 'spill', 'rld']
        sems = {k: es.enter_context(nc.semaphore(k)) for k in sem_names}
        block = es.enter_context(nc.Block())

        o_XR, o_HR, o_HPR, o_ACTR, o_WS, o_MISC = 0, 32768, 49152, 65536, 76800, 93184

        def bf(o, n):
            return A[:, o:o + n]

        def fp(o, n):
            return A[:, o:o + 2 * n].bitcast(F32)

        X = fp(o_XR, 16384).rearrange("p (c t) -> p c t", c=16)
        H = bf(o_HR, 16384).rearrange("p (c t) -> p c t", c=16)
        HP = bf(o_HPR, 16384).rearrange("p (c t) -> p c t", c=16)
        MERGED = HP
        ACTB = bf(o_ACTR, GJ * 1024).rearrange("p (c t) -> p c t", c=GJ)
        WSL = [bf(o_WS + i * SLOT, SLOT) for i in range(NS)]
        m = o_MISC

        def al(n):
            return (n + 127) // 128 * 128

        CST = bf(m, 384); m += al(384)
        ONES, IDENT, MASKNEG = CST[:, 0:128], CST[:, 128:256], CST[:, 256:384]
        ONESF = bf(m, 128); m += al(128)
        PRM = fp(m, NP); m += al(2 * NP)
        RS = [fp(m + i * 1024, 512) for i in range(2)]; m += 2048
        SQ = [bf(m + i * 512, 512) for i in range(4)]; m += 2048
        SG = [fp(m + i * 1024, 512) for i in range(2)]; m += 2048
        SC = fp(m, 16); m += al(32)
        LAMP = fp(m, 256); m += al(512)
        assert m <= 103 * 1024
        YA = bf(o_XR, 16384).rearrange("p (c t) -> p c t", c=16)
        YC = bf(o_XR + 16384, 16384).rearrange("p (c t) -> p c t", c=16)
        o = o_XR + 16384
        Qb = bf(o, 2048).rearrange("p (c t) -> p c t", c=2); o += 2048
        KT = bf(o, 4096).rearrange("p (c t) -> p c t", c=2); o += 4096
        Vb = bf(o, 4096).rearrange("p (c t) -> p c t", c=16); o += 4096
        o = o_ACTR
        OD = fp(o, 2048).rearrange("p (c t) -> p c t", c=2); o += 4096
        T1 = fp(o, 1024).rearrange("p (c t) -> p c t", c=2); o += 2048
        R1 = fp(o, 512); o += 1024
        R2 = fp(o, 512); o += 1024
        PT = [bf(o + i * 512, 512) for i in range(4)]; o += 2048
        o = o_ACTR
        Ub = fp(o, 1024); o += 2048
        CUH = fp(o, 1028); o += 2176
        Ab = fp(o, 1024); o += 2048
        HAL = fp(o, 4); o += 128
        o = o_ACTR
        SGA = fp(o, 1024); o += 2048
        SGC = fp(o, 1024); o += 2048
        M1 = fp(o, 1024); o += 2048
        M2 = fp(o, 1024); o += 2048
        MT = fp(o_HR, 4096).rearrange("p (c t) -> p c t", c=16)
        MH = bf(o_HR + 8192, 4096).rearrange("p (c t) -> p c t", c=16)
        QX = bf(o_HPR + 12288, 4096).rearrange("p (c t) -> p c t", c=4)
        o = o_ACTR
        KX = bf(o, 1024).rearrange("p (c t) -> p c t", c=4); o += 1024
        VX = bf(o, 1024).rearrange("p (c t) -> p c t", c=2); o += 1024
        OX = bf(o, 4096).rearrange("p (c t) -> p c t", c=4); o += 4096
        PTX = [bf(o + i * 512, 512) for i in range(4)]; o += 2048
        RX = fp(o, 512); o += 1024

        wr_tok = {}
        rd_tok = {}
        for b_ in range(8):
            wr_tok[b_] = None
            rd_tok[b_] = []
            for h_ in range(2):
                wr_tok[(b_, h_)] = None
                rd_tok[(b_, h_)] = []

        def MM(bank, out, lhsT, rhs, start, stop, waits=(), sig=None, sgc=False):
            ws = list(waits)
            if start:
                if isinstance(bank, tuple):
                    ws.append(list(rd_tok[bank]))
                    ws.append(list(rd_tok[bank[0]]))
                    rd_tok[bank] = []
                else:
                    for k_ in (bank, (bank, 0), (bank, 1)):
                        ws.append(list(rd_tok[k_]))
                        rd_tok[k_] = []
            if sig is None:
                sig = stop
            if sgc:
                fn = lambda e, o=out, l=lhsT, r=rhs, s=start, p=stop: e.matmul(o, l, r, start=s, stop=p, skip_group_check=True)
            else:
                fn = lambda e, o=out, l=lhsT, r=rhs, s=start, p=stop: e.matmul(o, l, r, start=s, stop=p)
            tok = S.op('pe', fn, ws, sig)
            if stop:
                wr_tok[bank] = tok
            return tok

        def EV(eng, fn, banks=(), waits=()):
            ws = list(waits) + [wr_tok[b] for b in banks]
            tok = S.op(eng, fn, ws, sig=True)
            for b in banks:
                rd_tok[b].append(tok)
            return tok

        def act_copy(out, in_):
            return lambda e, o=out, i=in_: e.activation(out=o, in_=i, func=AF.Copy)

        def act_fn(out, in_, func, **kw):
            return lambda e, o=out, i=in_, f=func, k=kw: e.activation(out=o, in_=i, func=f, **k)

        def dve_copy(out, in_):
            return lambda e, o=out, i=in_: e.tensor_copy(out=o, in_=i)

        def dve_tt(out, a, b, op):
            return lambda e, o=out, x=a, y=b, p=op: e.tensor_tensor(out=o, in0=x, in1=y, op=p)

        def dve_ts(out, a, s1, s2, op0, op1=None):
            if op1 is None:
                return lambda e, o=out, x=a, s=s1, p=op0: e.tensor_scalar(out=o, in0=x, scalar1=s, scalar2=None, op0=p)
            return lambda e, o=out, x=a, s=s1, s_2=s2, p=op0, q=op1: e.tensor_scalar(out=o, in0=x, scalar1=s, scalar2=s_2, op0=p, op1=q)

        def dve_stt(out, a, s, b, op0, op1):
            return lambda e, o=out, x=a, sc=s, y=b, p=op0, q=op1: e.scalar_tensor_tensor(out=o, in0=x, scalar=sc, in1=y, op0=p, op1=q)

        def dve_recip(out, in_):
            return lambda e, o=out, i=in_: e.reciprocal(out=o, in_=i)

        def barrier(engs=('pe', 'act', 'dve')):
            if not _os.environ.get("K_BARRIER"):
                return
            toks = [S.last.get(e) for e in ('pe', 'act', 'dve')]
            for e in engs:
                S.op(e, None, toks)

        wstate = {'i': 0, 'rel': {}}

        def fetch(key):
            off, size = OFF[key]
            i = wstate['i']
            wstate['i'] += 1
            slot = i % NS
            waits = []
            if i >= NS:
                waits.append(wstate['rel'][i - NS])
            tok = S.op('pool', lambda e, s=slot, o=off, n=size: e.dma_start(out=WSL[s][:, 0:n], in_=w_d[:, o:o + n]),
                       waits, dma_sem=f'w{slot}', wr=[WSL[slot][:, 0:size]])
            return i, WSL[slot], tok

        def release(i, tok):
            wstate['rel'][i] = tok

        def tsl(t):
            return slice(t * T, (t + 1) * T)

        t_prm = S.op('sp', lambda e: e.dma_start(out=PRM, in_=prm_d), dma_sem='ldp', wr=[PRM])
        t_cst = S.op('pool', lambda e: e.dma_start(out=CST, in_=cst_d), dma_sem='ldc', wr=[CST])
        tk = S.op('dve', dve_ts(ONESF, ONES, PRM[:, P_FLAG:P_FLAG + 1], None, ALU.mult), [t_prm, t_cst], sig=True)
        LAMV = PRM[:, P_LAM:P_LAM + 512].rearrange("p (c t) -> p c t", c=4)
        LP = LAMP.rearrange("p (c t) -> p c t", c=2)
        tk = S.op('dve', dve_tt(LP[:, 0, :], LAMV[:, 0, :], LAMV[:, 1, :], ALU.mult), [t_prm], sig=True)
        tk = S.op('dve', dve_tt(LP[:, 1, :], LAMV[:, 2, :], LAMV[:, 3, :], ALU.mult), [tk], sig=True)
        tk = S.op('dve', lambda e: e.reduce_sum(out=SC[:, 0:1], in_=LP[:, 0, :], axis=mybir.AxisListType.X), [tk], sig=True, rd=[LAMP], wr=[SC])
        tk = S.op('dve', lambda e: e.reduce_sum(out=SC[:, 1:2], in_=LP[:, 1, :], axis=mybir.AxisListType.X), [tk], sig=True, rd=[LAMP], wr=[SC])
        tk = S.op('act', act_fn(SC[:, 2:4], SC[:, 0:2], AF.Exp), [tk], sig=True)
        tk = S.op('dve', dve_tt(SC[:, 4:5], SC[:, 2:3], SC[:, 3:4], ALU.subtract), [tk], sig=True)
        tk = S.op('dve', dve_ts(SC[:, 5:6], SC[:, 4:5], 0.2, -1.0, ALU.add, ALU.mult), [tk], sig=True)
        NEGLAM = SC[:, 5:6]
        t_setup = tk

        def rmsnorm(src, nchunk, ntile, tw, gcol, dst, eps, ndim, waits, stat_banks=(6, 7), post_scale=None, cwaits=None, ctoks=None):
            first = True
            last = None
            tile_toks = []
            for t in range(ntile):
                sl = slice(t * tw, (t + 1) * tw)
                bank = stat_banks[t % 2]
                for c in range(nchunk):
                    w = [waits] if first else []
                    first = False
                    if cwaits is not None and t == 0:
                        w.append(cwaits[c])
                    k = c % 4
                    tsq = S.op('act', act_fn(SQ[k][:, 0:tw], src[:, c, sl], AF.Square), w + [sq_rd[k]], sig=True)
                    sq_rd[k] = MM(bank, PS[bank][:, 0:tw], ONES, SQ[k][:, 0:tw], c == 0, c == nchunk - 1, [tsq], sig=True)
                r = RS[t % 2][:, 0:tw]
                t1 = EV('dve', dve_ts(r, PS[bank][:, 0:tw], 1.0 / ndim, eps, ALU.mult, ALU.add), [bank], [rs_rd[t % 2]])
                t2 = S.op('act', act_fn(r, r, AF.Sqrt), [t1], sig=True)
                t3 = S.op('dve', dve_recip(r, r), [t2], sig=True)
                if post_scale is not None:
                    t3 = S.op('dve', dve_ts(r, r, post_scale, None, ALU.mult), [t3], sig=True)
                for c in range(nchunk):
                    last = S.op('dve', dve_stt(dst[:, c, sl], src[:, c, sl], PRM[:, gcol + c:gcol + c + 1], r, ALU.mult, ALU.mult),
                                [t3], sig=True)
                    if ctoks is not None:
                        ctoks[c] = last
                rs_rd[t % 2] = last
                tile_toks.append(last)
            return tile_toks

        sq_rd = [None] * 4
        rs_rd = [None] * 2

        def ffn(f, h_ready):
            sg_rd = [None, None]
            last_down_pe = None
            x_tok = None
            for g in range(NG):
                nj = min(GJ, NJ - GJ * g)
                act_tok = None
                for jj in range(nj):
                    j = g * GJ + jj
                    bi, blk, ltok = fetch(('gu', f, j))
                    bv = blk.rearrange("p (a k m) -> p a k m", a=2, k=16)
                    tg = [None, None]
                    tu = [None, None]
                    if j == 0:
                        for t in range(2):
                            for kc in range(16):
                                tg[t] = MM(t, PS[t], bv[:, 0, kc, :], H[:, kc, tsl(t)], kc == 0, kc == 15,
                                           [ltok, h_ready[t]] if kc == 0 else [])
                            for kc in range(16):
                                tu[t] = MM(2 + t, PS[2 + t], bv[:, 1, kc, :], H[:, kc, tsl(t)], kc == 0, kc == 15)
                    else:
                        for kc in range(16):
                            for t in range(2):
                                tg[t] = MM(t, PS[t], bv[:, 0, kc, :], H[:, kc, tsl(t)], kc == 0, kc == 15,
                                           [ltok] if kc == 0 else [])
                        for kc in range(16):
                            for t in range(2):
                                tu[t] = MM(2 + t, PS[2 + t], bv[:, 1, kc, :], H[:, kc, tsl(t)], kc == 0, kc == 15)
                    release(bi, tu[1])
                    for t in range(2):
                        ts_ = EV('act', act_fn(SG[t], PS[t], AF.Silu), [t], [sg_rd[t]])
                        act_tok = EV('dve', dve_tt(ACTB[:, jj, tsl(t)], PS[2 + t], SG[t], ALU.mult), [2 + t],
                                     [ts_, last_down_pe])
                        sg_rd[t] = act_tok
                for c in range(16):
                    bi, blk, ltok = fetch(('dn', f, g, c))
                    bv = blk[:, 0:nj * 128].rearrange("p (j m) -> p j m", j=nj)
                    tks = [None, None]
                    for jj in range(nj):
                        for t in range(2):
                            bank = 4 + 2 * (c % 2) + t
                            tks[t] = MM(bank, PS[bank], bv[:, jj, :], ACTB[:, jj, tsl(t)], jj == 0, jj == nj - 1,
                                        [ltok] if jj == 0 else [])
                    release(bi, tks[1])
                    last_down_pe = tks[1]
                    for t in range(2):
                        bank = 4 + 2 * (c % 2) + t
                        x_tok = EV('dve', dve_stt(X[:, c, tsl(t)], PS[bank], 0.5, X[:, c, tsl(t)], ALU.mult, ALU.add), [bank])
            return x_tok

        def load_x(src_d):
            i = 0
            for t in range(2):
                for g4 in range(4):
                    cs = slice(4 * g4, 4 * g4 + 4)
                    S.op('sp', lambda e, cs=cs, t=t: e.dma_start(out=X[:, cs, tsl(t)], in_=src_d[cs, :, tsl(t)].rearrange("c p t -> p c t")),
                         [], dma_sem=f'ldx{i}', wr=[X[:, cs, tsl(t)]])
                    i += 1

        load_x(xp_d)
        th = rmsnorm(X, 16, 2, T, P_FFN1, H, 1e-6, D, [t_setup])
        tx1 = ffn(1, th)
        barrier()
        thp = rmsnorm(X, 16, 2, T, P_MIX, HP, 1e-6, D, [])
        load_x(xo_d)
        th = rmsnorm(X, 16, 2, T, P_FFN1, H, 1e-6, D, [])
        tx1 = ffn(1, th)
        barrier()
        th = rmsnorm(X, 16, 2, T, P_MIX, H, 1e-6, D, [])
        barrier()
        t_spill = []
        for i_, c0_ in enumerate((8, 10, 12, 14, 0, 2, 4, 6)):
            cs_ = slice(c0_, c0_ + 2)
            t_spill.append(S.op('sp', lambda e, cs=cs_: e.dma_start(out=xs_d[cs].rearrange("c p t -> p c t"), in_=X[:, cs, :]),
                                [], dma_sem=f'ldx{i_}', rd=[X[:, cs_, :]]))

        TA = 256
        PT8 = [PT[i // 2][:, (i % 2) * 256:(i % 2 + 1) * 256] for i in range(8)]

        def head_proj_q(hd):
            bi, blk, ltok = fetch(('q', hd))
            bv = blk.rearrange("p (k m) -> p k m", k=16)
            if hd == 0:
                tk = None
                for t in range(2):
                    for comp in range(2):
                        bank = 2 * comp + t
                        for kc in range(16):
                            tk = MM(bank, PS[bank], bv[:, kc, comp * 128:(comp + 1) * 128], H[:, kc, tsl(t)],
                                    kc == 0, kc == 15, [ltok] if kc == 0 else [])
                        EV('act', act_copy(Qb[:, comp, tsl(t)], PS[bank]), [bank])
                release(bi, tk)
                return
            for comp in range(2):
                banks = (0, 1) if comp == 0 else (2, 3)
                tk_ = [None, None]
                for kc in range(16):
                    for t in range(2):
                        tk_[t] = MM(banks[t], PS[banks[t]], bv[:, kc, comp * 128:(comp + 1) * 128], H[:, kc, tsl(t)],
                                    kc == 0, kc == 15, [ltok] if kc == 0 else [])
                if comp == 1:
                    release(bi, tk_[1])
                for t in range(2):
                    EV('act', act_copy(Qb[:, comp, tsl(t)], PS[banks[t]]), [banks[t]])

        def head_proj_kv(hd):
            bi, blk, ltok = fetch(('k', hd))
            bv = blk.rearrange("p (k m) -> p k m", k=16)
            for comp in range(2):
                banks = (0, 1, 5, 6) if comp == 0 else (2, 3, 4, 7)
                tk_ = [None] * 4
                for kc in range(16):
                    for tt in range(4):
                        src = HP if tt < 2 else H
                        tk_[tt] = MM(banks[tt], PS[banks[tt]], bv[:, kc, comp * 128:(comp + 1) * 128], src[:, kc, tsl(tt % 2)],
                                     kc == 0, kc == 15, [ltok] if kc == 0 else [])
                if comp == 1:
                    release(bi, tk_[3])
                for tt in range(4):
                    eng = 'dve' if tt % 2 == 0 else 'act'
                    fn = dve_copy(KT[:, comp, tsl(tt)], PS[banks[tt]]) if eng == 'dve' else act_copy(KT[:, comp, tsl(tt)], PS[banks[tt]])
                    EV(eng, fn, [banks[tt]])
            bi, blk, ltok = fetch(('v', hd))
            bv = blk.rearrange("p (k m) -> p k m", k=16)
            tk_ = None
            for tc in range(16):
                src = HP if tc < 8 else H
                tcl = tc % 8
                bank = 4 + (tc % 4)
                for kc in range(16):
                    tk_ = MM(bank, PS[bank][:, 0:256], src[:, kc, tcl * 128:(tcl + 1) * 128], bv[:, kc, :],
                             kc == 0, kc == 15, [ltok] if kc == 0 else [])
                if tc < 8:
                    EV('dve', dve_ts(Vb[:, tc, :], PS[bank][:, 0:256], PRM[:, P_FLAG:P_FLAG + 1], None, ALU.mult), [bank])
                else:
                    EV('act', act_copy(Vb[:, tc, :], PS[bank][:, 0:256]), [bank])
            release(bi, tk_)
            return [S.last['act'], S.last['dve']]

        def head_attn(hd, q_ready):
            steps = []
            for qt in range(4):
                for comp in range(2):
                    lst = [(kc, 0, False, True) for kc in range(8)]
                    for oc in range(2 * qt + 2):
                        if oc < 2 * qt:
                            lst.append((8 + oc, 0, False, False))
                        else:
                            lst.append((8 + oc, (oc - 2 * qt) * 128, True, False))
                    for i, (kcg, col0, diag, prev) in enumerate(lst):
                        steps.append(dict(qt=qt, comp=comp, kcg=kcg, col0=col0, diag=diag, prev=prev,
                                          first=(i == 0), last=(i == len(lst) - 1)))
            n = len(steps)
            exp_tok = [None] * n
            pv_tok = [None] * n
            PD = 3

            def sview(s, a, b_):
                bank = s % 4
                return bank, PS[bank][:, a:b_]

            def qk(s):
                st = steps[s]
                c0 = st['col0']
                key, out = sview(s, c0, 256)
                w = []
                tk_ = MM(key, out, KT[:, st['comp'], st['kcg'] * 128:(st['kcg'] + 1) * 128],
                         Qb[:, st['comp'], st['qt'] * TA + c0:(st['qt'] + 1) * TA], True, not st['diag'], w)
                if st['diag']:
                    key, out = sview(s, c0, c0 + 128)
                    tk_ = MM(key, out, IDENT, MASKNEG, False, True)
                return tk_

            def ex(s):
                st = steps[s]
                c0 = st['col0']
                key, src = sview(s, c0, 256)
                w = [pv_tok[s - 8]] if s >= 8 else []
                exp_tok[s] = EV('act', act_fn(PT8[s % 8][:, c0:256], src, AF.Exp, scale=SCALE), [key], w)

            def pv(s):
                st = steps[s]
                c0 = st['col0']
                comp = st['comp']
                qt = st['qt']
                ob = 4 if comp == 0 else 6
                lb = 5 if comp == 0 else 7
                p = PT8[s % 8][:, c0:256]
                for dvc in range(2):
                    MM(ob, PS[ob][:, dvc * 256 + c0:(dvc + 1) * 256], Vb[:, st['kcg'], dvc * 128:(dvc + 1) * 128], p,
                       st['first'] and dvc == 0, st['last'], [exp_tok[s]] if dvc == 0 else [], sgc=True)
                pv_tok[s] = MM(lb, PS[lb][:, c0:256], ONESF if st['prev'] else ONES, p, st['first'], st['last'], sig=True)
                if st['last']:
                    qs = slice(qt * TA, (qt + 1) * TA)
                    if comp == 0:
                        a = EV('dve', dve_recip(R1[:, 0:256], PS[5][:, 0:256]), [5])
                        for dvc in range(2):
                            EV('dve', dve_tt(T1[:, dvc, 0:256], PS[4][:, dvc * 256:(dvc + 1) * 256], R1[:, 0:256], ALU.mult), [4], [a])
                    else:
                        a = EV('dve', dve_recip(R2[:, 0:256], PS[7][:, 0:256]), [7])
                        a = S.op('dve', dve_ts(R2[:, 0:256], R2[:, 0:256], NEGLAM, None, ALU.mult), [a], sig=True)
                        for dvc in range(2):
                            b = EV('dve', dve_tt(OD[:, dvc, qs], PS[6][:, dvc * 256:(dvc + 1) * 256], R2[:, 0:256], ALU.mult), [6], [a])
                            S.op('dve', dve_tt(OD[:, dvc, qs], OD[:, dvc, qs], T1[:, dvc, 0:256], ALU.add), [b], sig=True)

            for s in range(min(PD, n)):
                qk(s)
                ex(s)
            for s in range(n):
                if s + PD < n:
                    qk(s + PD)
                    ex(s + PD)
                pv(s)
            return S.last['dve']

        def head_subln_sq(od_tok):
            toks = {}
            for t in range(2):
                for dvc in range(2):
                    k = 2 * t + dvc
                    toks[(t, dvc)] = S.op('act', act_fn(SQ[k], OD[:, dvc, tsl(t)], AF.Square), [od_tok, sq_rd[k]], sig=True)
            return toks

        def head_subln(hd, sqt):
            for t in range(2):
                bank = 4 if t == 0 else 7
                for dvc in range(2):
                    k = 2 * t + dvc
                    sq_rd[k] = MM(bank, PS[bank], ONES, SQ[k], dvc == 0, dvc == 1, [sqt[(t, dvc)]], sig=True)
                r = RS[t]
                t1 = EV('dve', dve_ts(r, PS[bank], 1.0 / 256, 1e-5, ALU.mult, ALU.add), [bank], [rs_rd[t]])
                t2 = S.op('act', act_fn(r, r, AF.Sqrt), [t1], sig=True)
                t3 = S.op('dve', dve_recip(r, r), [t2], sig=True)
                t3 = S.op('dve', dve_ts(r, r, 0.8, None, ALU.mult), [t3], sig=True)
                for dvc in range(2):
                    rs_rd[t] = S.op('dve', dve_stt(YA[:, 2 * hd + dvc, tsl(t)], OD[:, dvc, tsl(t)],
                                                   PRM[:, P_SUB + dvc:P_SUB + dvc + 1], r, ALU.mult, ALU.mult), [t3], sig=True)

        sqt = None
        for hd in range(8):
            head_proj_q(hd)
            if hd > 0:
                head_subln(hd - 1, sqt)
            qr = head_proj_kv(hd)
            od_tok = head_attn(hd, qr)
            sqt = head_subln_sq(od_tok)
        head_subln(7, sqt)
        barrier(engs=('act', 'dve'))

        CU = CUH[:, 2:1026]
        for c in range(16):
            tkc = {}
            for nm, b0, hb in (('cC', 0, 6), ('cU', 2, 7), ('cB', 4, None)):
                bi, blk, ltok = fetch((nm, c))
                bv = blk[:, 0:2048].rearrange("p (k m) -> p k m", k=16)
                tk_ = [None, None]
                for kc in range(16):
                    for t in range(2):
                        tk_[t] = MM(b0 + t, PS[b0 + t], bv[:, kc, :], H[:, kc, tsl(t)], kc == 0, kc == 15,
                                    [ltok] if kc == 0 else [])
                if hb is not None:
                    for kc in range(16):
                        tk_[1] = MM(hb, PS[hb][:, 0:2], bv[:, kc, :], HP[:, kc, 1022:1024], kc == 0, kc == 15)
                release(bi, tk_[1])
            tu_ = None
            for t in range(2):
                tu_ = EV('act', act_copy(Ub[:, tsl(t)], PS[2 + t]), [2 + t], [conv_rd] if c > 0 and t == 0 else [])
            th_ = EV('act', act_copy(HAL[:, 0:2], PS[7][:, 0:2]), [7])
            a = None
            for t in range(2):
                a = EV('dve', dve_tt(CU[:, tsl(t)], PS[t], Ub[:, tsl(t)], ALU.mult), [t], [tu_])
            a = EV('dve', dve_tt(CUH[:, 0:2], PS[6][:, 0:2], HAL[:, 0:2], ALU.mult), [6], [th_, a])
            a = S.op('dve', dve_ts(CUH[:, 0:2], CUH[:, 0:2], PRM[:, P_FLAG:P_FLAG + 1], None, ALU.mult), [a], sig=True)
            w0 = PRM[:, P_CW + c:P_CW + c + 1]
            w1 = PRM[:, P_CW + 16 + c:P_CW + 16 + c + 1]
            w2 = PRM[:, P_CW + 32 + c:P_CW + 32 + c + 1]
            a = S.op('dve', dve_ts(Ab, CUH[:, 0:1024], w0, None, ALU.mult), [a], sig=True)
            a = S.op('dve', dve_stt(Ab, CUH[:, 1:1025], w1, Ab, ALU.mult, ALU.add), [a], sig=True)
            a = S.op('dve', dve_stt(Ab, CUH[:, 2:1026], w2, Ab, ALU.mult, ALU.add), [a], sig=True)
            for t in range(2):
                a = EV('dve', dve_tt(YC[:, c, tsl(t)], PS[4 + t], Ab[:, tsl(t)], ALU.mult), [4 + t], [a])
            conv_rd = a
        barrier()

        mrd = None
        for c in range(16):
            specs = (('ga', H, 0), ('gc', H, 2), ('ao', YA, 4), ('co', YC, 6))
            for nm, src, b0 in specs:
                bi, blk, ltok = fetch((nm, c))
                bv = blk[:, 0:2048].rearrange("p (k m) -> p k m", k=16)
                tk_ = [None, None]
                for kc in range(16):
                    for t in range(2):
                        tk_[t] = MM(b0 + t, PS[b0 + t], bv[:, kc, :], src[:, kc, tsl(t)], kc == 0, kc == 15,
                                    [ltok] if kc == 0 else [])
                release(bi, tk_[1])
            s1 = s2 = None
            for t in range(2):
                s1 = EV('act', act_fn(SGA[:, tsl(t)], PS[0 + t], AF.Sigmoid, bias=PRM[:, P_BGA + c:P_BGA + c + 1]), [0 + t],
                        [mrd] if t == 0 else [])
                s2 = EV('act', act_fn(SGC[:, tsl(t)], PS[2 + t], AF.Sigmoid, bias=PRM[:, P_BGC + c:P_BGC + c + 1]), [2 + t])
            for t in range(2):
                a = EV('dve', dve_tt(M1[:, tsl(t)], PS[4 + t], SGA[:, tsl(t)], ALU.mult), [4 + t], [s1, s2])
                b = EV('dve', dve_tt(M2[:, tsl(t)], PS[6 + t], SGC[:, tsl(t)], ALU.mult), [6 + t], [a])
                mrd = S.op('dve', dve_tt(MERGED[:, c, tsl(t)], M1[:, tsl(t)], M2[:, tsl(t)], ALU.add), [b], sig=True)
        barrier()
        t_rld = [S.op('sp', lambda e, q=q: e.dma_start(out=X[:, 4 * q:4 * q + 4, :], in_=xs_d[4 * q:4 * q + 4].rearrange("c p t -> p c t")),
                      [S.last['pe'], S.last['dve'], S.last['act'], t_spill], dma_sem=f'ldx{q}', wr=[X[:, 4 * q:4 * q + 4, :]]) for q in range(4)]
        t_mem = S.op('sp', lambda e: e.dma_start(out=MT, in_=mem_d.rearrange("c p t -> p c t")), [], dma_sem='ldm', wr=[MT])
        xt = None
        for c in range(16):
            if c == 8:
                tmh = rmsnorm(MT, 16, 1, 256, P_MEM, MH, 1e-6, D, [t_mem])
            bi, blk, ltok = fetch(('mo', c))
            bv = blk[:, 0:2048].rearrange("p (k m) -> p k m", k=16)
            tk_ = [None, None]
            b0 = 2 * (c % 4)
            for kc in range(16):
                for t in range(2):
                    tk_[t] = MM(b0 + t, PS[b0 + t], bv[:, kc, :], MERGED[:, kc, tsl(t)], kc == 0, kc == 15,
                                [ltok] if kc == 0 else [])
            release(bi, tk_[1])
            for t in range(2):
                xt = EV('dve', dve_tt(X[:, c, tsl(t)], PS[b0 + t], X[:, c, tsl(t)], ALU.add), [b0 + t], [t_rld[c // 4]])
        barrier()

        for hd in range(4):
            bi, blk, ltok = fetch(('xk', hd))
            bv = blk[:, 0:2048].rearrange("p (k m) -> p k m", k=16)
            tk_ = None
            bank = hd % 2
            for kc in range(16):
                tk_ = MM(bank, PS[bank][:, 0:256], bv[:, kc, :], MH[:, kc, :], kc == 0, kc == 15, [ltok] if kc == 0 else [])
            release(bi, tk_)
            EV('act', act_copy(KX[:, hd, :], PS[bank][:, 0:256]), [bank])
        for i in range(2):
            bi, blk, ltok = fetch(('xv', i))
            bv = blk.rearrange("p (k m) -> p k m", k=16)
            tk_ = None
            for mc in range(2):
                bank = 2 + mc
                for kc in range(16):
                    tk_ = MM(bank, PS[bank][:, 0:256], MH[:, kc, mc * 128:(mc + 1) * 128], bv[:, kc, :], kc == 0, kc == 15,
                             [ltok] if kc == 0 else [])
                EV('dve', dve_copy(VX[:, mc, i * 256:(i + 1) * 256], PS[bank][:, 0:256]), [bank])
            release(bi, tk_)
        th = rmsnorm(X, 16, 2, T, P_XA, H, 1e-6, D, [])
        for hd in range(4):
            bi, blk, ltok = fetch(('xq', hd))
            bv = blk[:, 0:2048].rearrange("p (k m) -> p k m", k=16)
            tk_ = [None, None]
            b0 = 4 + 2 * (hd % 2)
            for kc in range(16):
                for t in range(2):
                    tk_[t] = MM(b0 + t, PS[b0 + t], bv[:, kc, :], H[:, kc, tsl(t)], kc == 0, kc == 15, [ltok] if kc == 0 else [])
            release(bi, tk_[1])
            for t in range(2):
                EV('act' if t == 0 else 'dve',
                   act_copy(QX[:, hd, tsl(t)], PS[b0 + t]) if t == 0 else dve_copy(QX[:, hd, tsl(t)], PS[b0 + t]), [b0 + t])
        barrier()
        xsteps = [(hd, t, mc) for hd in range(4) for t in range(2) for mc in range(2)]
        nx = len(xsteps)
        x_exp = [None] * nx
        x_pv = [None] * nx

        def x_qk(si):
            hd, t, mc = xsteps[si]
            sb = si % 4
            MM(sb, PS[sb], KX[:, hd, mc * 128:(mc + 1) * 128], QX[:, hd, tsl(t)], True, True)
            x_exp[si] = EV('act', act_fn(PTX[si % 4], PS[sb], AF.Exp, scale=SCALE), [sb],
                           [x_pv[si - 4]] if si >= 4 else [])

        def x_pvf(si):
            hd, t, mc = xsteps[si]
            ob = 4 + ((si // 2) % 2) * 2
            p = PTX[si % 4]
            MM(ob, PS[ob], VX[:, mc, hd * 128:(hd + 1) * 128], p, mc == 0, mc == 1, [x_exp[si]])
            x_pv[si] = MM(ob + 1, PS[ob + 1], ONES, p, mc == 0, mc == 1, sig=True)
            if mc == 1:
                a = EV('dve', dve_recip(RX, PS[ob + 1]), [ob + 1])
                EV('dve', dve_tt(OX[:, hd, tsl(t)], PS[ob], RX, ALU.mult), [ob], [a])

        XPD = 2
        for si in range(min(XPD, nx)):
            x_qk(si)
        for si in range(nx):
            if si + XPD < nx:
                x_qk(si + XPD)
            x_pvf(si)
        blks = [fetch(('xo', i)) for i in range(2)]
        for c in range(16):
            bi, blk, ltok = blks[c // 8]
            bv = blk.rearrange("p (k m) -> p k m", k=4)
            b0 = 2 * (c % 4)
            tk_ = [None, None]
            for kc in range(4):
                for t in range(2):
                    tk_[t] = MM(b0 + t, PS[b0 + t], bv[:, kc, (c % 8) * 128:(c % 8 + 1) * 128], OX[:, kc, tsl(t)], kc == 0, kc == 3,
                                [ltok] if kc == 0 else [])
            if c % 8 == 7:
                release(bi, tk_[1])
            for t in range(2):
                xt = EV('dve', dve_tt(X[:, c, tsl(t)], PS[b0 + t], X[:, c, tsl(t)], ALU.add), [b0 + t])
        barrier()

        th = rmsnorm(X, 16, 2, T, P_FFN2, H, 1e-6, D, [])
        tx2 = ffn(2, th)
        barrier()
        tf = rmsnorm(X, 16, 2, T, P_FIN, X, 1e-6, D, [])
        t_out = None
        for t in range(2):
            for g4 in range(4):
                cs = slice(4 * g4, 4 * g4 + 4)
                t_out = S.op('sp', lambda e, cs=cs, t=t: e.dma_start(out=out_d[cs, :, tsl(t)].rearrange("c p t -> p c t"), in_=X[:, cs, tsl(t)]),
                             [], dma_sem='st', rd=[X[:, cs, tsl(t)]])
        S.op('sp', None, [t_out])

        def replay(name, eng):
            seen = {}
            for fn, waits, tok, is_dma in S.ops[name]:
                for (k, v) in waits:
                    if seen.get(k, 0) >= v:
                        continue
                    eng.wait_ge(sems[k], v)
                    seen[k] = v
                if fn is None:
                    continue
                ins = fn(eng)
                if tok is not None:
                    ins.then_inc(sems[tok[0]], 16 if is_dma else 1)

        @block.tensor
        def _(e):
            replay('pe', e)

        @block.scalar
        def _(e):
            replay('act', e)

        @block.vector
        def _(e):
            replay('dve', e)

        @block.gpsimd
        def _(e):
            replay('pool', e)

        @block.sync
        def _(e):
            replay('sp', e)
    return nc


def _blk_kn(W, n0, n1):
    K = W.shape[0]
    kc = K // 128
    n = n1 - n0
    return np.ascontiguousarray(W[:, n0:n1].reshape(kc, 128, n).transpose(1, 0, 2)).reshape(128, kc * n)


def _host_weights(inp):
    OFF, TOTAL = plan()
    wb = np.empty((128, TOTAL), np.float32)

    def put(key, arr):
        o, s = OFF[key]
        assert arr.shape == (128, s), (key, arr.shape, s)
        wb[:, o:o + s] = arr

    for f in (1, 2):
        Wg, Wu, Wd = inp[f'ffn{f}_w_gate'][0], inp[f'ffn{f}_w_up'][0], inp[f'ffn{f}_w_down'][0]
        for j in range(NJ):
            put(('gu', f, j), np.concatenate([_blk_kn(Wg, j * 128, (j + 1) * 128), _blk_kn(Wu, j * 128, (j + 1) * 128)], axis=1))
        for g in range(NG):
            nj = min(GJ, NJ - GJ * g)
            rows = Wd[g * GJ * 128:(g * GJ + nj) * 128]
            for c in range(16):
                put(('dn', f, g, c), _blk_kn(rows, c * 128, (c + 1) * 128))
    Wmi = inp['w_mix_in'][0]
    for hd in range(8):
        put(('q', hd), _blk_kn(Wmi, hd * 256, hd * 256 + 256))
        put(('k', hd), _blk_kn(Wmi, 2048 + hd * 256, 2048 + hd * 256 + 256))
        put(('v', hd), _blk_kn(Wmi, 4096 + hd * 256, 4096 + hd * 256 + 256))
    for c in range(16):
        for nm, base in (('cB', 6144), ('cC', 8192), ('cU', 10240), ('ga', 12288), ('gc', 14336)):
            put((nm, c), _blk_kn(Wmi, base + c * 128, base + (c + 1) * 128))
        put(('ao', c), _blk_kn(inp['w_attn_out'][0], c * 128, (c + 1) * 128))
        put(('co', c), _blk_kn(inp['w_conv_out'][0], c * 128, (c + 1) * 128))
        put(('mo', c), _blk_kn(inp['w_mix_out'][0], c * 128, (c + 1) * 128))
    for hd in range(4):
        put(('xq', hd), _blk_kn(inp['w_xq'][0], hd * 128, (hd + 1) * 128))
        put(('xk', hd), _blk_kn(inp['w_xkv'][0], hd * 128, (hd + 1) * 128))
    for i in range(2):
        put(('xv', i), _blk_kn(inp['w_xkv'][0], 512 + i * 256, 512 + (i + 1) * 256))
        put(('xo', i), _blk_kn(inp['w_xo'][0], i * 1024, (i + 1) * 1024))
    return wb


def _colmajor16(v):
    return np.ascontiguousarray(np.asarray(v, np.float32).reshape(16, 128).T)


def kernel(**inp):
    inp = {k: np.asarray(v) for k, v in inp.items()}
    x = inp['x'].astype(np.float32, copy=False)
    mem = inp['mem'].astype(np.float32, copy=False)
    wb = _host_weights(inp)
    prm = np.zeros((128, NP), np.float32)
    prm[:, P_FFN1:P_FFN1 + 16] = _colmajor16(inp['ffn1_norm'][0])
    prm[:, P_MIX:P_MIX + 16] = _colmajor16(inp['mix_norm'][0])
    prm[:, P_XA:P_XA + 16] = _colmajor16(inp['xattn_norm'][0])
    prm[:, P_MEM:P_MEM + 16] = _colmajor16(inp['mem_norm'][0])
    prm[:, P_FFN2:P_FFN2 + 16] = _colmajor16(inp['ffn2_norm'][0])
    prm[:, P_FIN:P_FIN + 16] = _colmajor16(inp['final_norm'])
    prm[:, P_BGA:P_BGA + 16] = _colmajor16(inp['b_gates'][0, 0])
    prm[:, P_BGC:P_BGC + 16] = _colmajor16(inp['b_gates'][0, 1])
    for j in range(3):
        prm[:, P_CW + 16 * j:P_CW + 16 * j + 16] = _colmajor16(inp['conv_w'][0, j])
    prm[:, P_SUB:P_SUB + 2] = np.asarray(inp['diff_subln'][0], np.float32).reshape(2, 128).T
    for i, nm in enumerate(('lambda_q1', 'lambda_k1', 'lambda_q2', 'lambda_k2')):
        prm[:, P_LAM + 128 * i:P_LAM + 128 * (i + 1)] = np.asarray(inp[nm][0], np.float32)[None, :]
    cst = np.zeros((128, 384), np.float32)
    cst[:, 0:128] = 1.0
    cst[:, 128:256] = np.eye(128, dtype=np.float32)
    kk = np.arange(128)[:, None]
    qq = np.arange(128)[None, :]
    cst[:, 256:384] = np.where(kk > qq, -30000.0, 0.0)

    def tr(a):
        return np.ascontiguousarray(a.T).reshape(16, 128, a.shape[0])

    in_maps = []
    for core in range(8):
        b, half = core // 2, core % 2
        p = prm.copy()
        p[:, P_FLAG] = float(half)
        in_maps.append({
            "x_own": tr(x[b, half * TOK:(half + 1) * TOK]),
            "x_prev": tr(x[b, 0:TOK]) if half == 1 else np.zeros((16, 128, TOK), np.float32),
            "memT": tr(mem[b]),
            "prm": p,
            "cst": cst,
            "wbig": wb,
        })
    nc = build()
    res = run_bass_kernel_spmd(nc, in_maps, core_ids=list(range(8)))
    out = np.empty((4, 2048, D), np.float32)
    for core in range(8):
        b, half = core // 2, core % 2
        o = np.asarray(res.results[core]["out"]).reshape(D, TOK)
        out[b, half * TOK:(half + 1) * TOK, :] = o.T
    return out
```

```python
import numpy as np
import os as _os
from contextlib import ExitStack
import concourse.bass as bass
import concourse.mybir as mybir
from concourse.bass_utils import run_bass_kernel_spmd

F32 = mybir.dt.float32
BF16 = mybir.dt.bfloat16
AF = mybir.ActivationFunctionType
ALU = mybir.AluOpType

D = 2048
DFF = 5504
NJ = 43
GJ = 11
NG = 4
TOK = 1024
T = 512
NS = 4
SLOT = 4096
SCALE = 128 ** -0.5

P_FFN1, P_MIX, P_XA, P_MEM, P_FFN2, P_FIN = 0, 16, 32, 48, 64, 80
P_BGA, P_BGC = 96, 112
P_CW = 128
P_SUB = 176
P_FLAG = 178
P_LAM = 180
NP = P_LAM + 512


def plan():
    keys = []
    for f in (1, 2):
        for j in range(NJ):
            keys.append((('gu', f, j), 4096))
        for g in range(NG):
            nj = min(GJ, NJ - GJ * g)
            for c in range(16):
                keys.append((('dn', f, g, c), nj * 128))
    for hd in range(8):
        for nm in 'qkv':
            keys.append(((nm, hd), 4096))
    for c in range(16):
        for nm in ('cB', 'cC', 'cU', 'ga', 'gc', 'ao', 'co', 'mo'):
            keys.append(((nm, c), 2048))
    for hd in range(4):
        keys.append((('xq', hd), 2048))
        keys.append((('xk', hd), 2048))
    for i in range(2):
        keys.append((('xv', i), 4096))
        keys.append((('xo', i), 4096))
    off = {}
    o = 0
    for k, s in keys:
        off[k] = (o, s)
        o += s
    return off, o


def _is_ap(x):
    return hasattr(x, 'offset') and hasattr(x, 'space') and hasattr(x, 'ap')


class Sched:
    G = 256

    def __init__(self):
        self.ops = {e: [] for e in ('pe', 'act', 'dve', 'pool', 'sp')}
        self.cnt = {}
        self.last = {}
        self.W = {}
        self.R = {}

    def _cells(self, ap):
        if 'DRAM' in str(ap.space):
            return ()
        dsz = mybir.dt.size(ap.dtype)
        dims = [(st, n) for st, n in list(ap.ap)[1:] if n > 1]
        nm = ap.tensor.name
        run = 1
        while dims and dims[-1][0] == run:
            run *= dims[-1][1]
            dims.pop()
        outer = 1
        for _, n in dims:
            outer *= n
        starts = [ap.offset]
        if outer <= 256:
            for st, n in dims:
                starts = [b + i * st for b in starts for i in range(n)]
            ext = run
        else:
            ext = run
            for st, n in dims:
                ext += (n - 1) * abs(st)
        cells = set()
        for b in starts:
            lo = b * dsz
            hi = lo + ext * dsz
            cells.update(range(lo // self.G, (hi - 1) // self.G + 1))
        return [(nm, c) for c in sorted(cells)]

    def op(self, eng, fn, waits=(), sig=False, dma_sem=None, rd=(), wr=()):
        rds, wrs = list(rd), list(wr)
        if fn is not None and fn.__defaults__:
            first = True
            for dflt in fn.__defaults__:
                if _is_ap(dflt):
                    (wrs if first else rds).append(dflt)
                    first = False
                elif isinstance(dflt, dict):
                    rds.extend(v for v in dflt.values() if _is_ap(v))
        tok = None
        if dma_sem is not None:
            self.cnt[dma_sem] = self.cnt.get(dma_sem, 0) + 16
            tok = (dma_sem, self.cnt[dma_sem])
            mark = tok
        elif sig:
            self.cnt[eng] = self.cnt.get(eng, 0) + 1
            tok = (eng, self.cnt[eng])
            self.last[eng] = tok
            mark = tok
        else:
            mark = (eng, self.cnt.get(eng, 0) + 1)
            assert eng == 'pe' or fn is None, "non-signalling op on a non-PE engine"
        ws = []
        for w in waits:
            if w is None:
                continue
            if isinstance(w, list):
                ws.extend([x for x in w if x is not None])
            else:
                ws.append(w)
        auto = {}
        rcells = [c for a in rds for c in self._cells(a)]
        wcells = [c for a in wrs for c in self._cells(a)]
        for c in rcells:
            for k, v in self.W.get(c, {}).items():
                auto[k] = max(auto.get(k, 0), v)
        for c in wcells:
            for d in (self.W.get(c, {}), self.R.get(c, {})):
                for k, v in d.items():
                    auto[k] = max(auto.get(k, 0), v)
        for k, v in auto.items():
            if eng == 'pe' and k == 'pe':
                continue
            if _os.environ.get("K_NOSAME") and k == eng:
                continue
            if _os.environ.get("K_NOAUTO"):
                continue
            if k == 'pe':
                assert v <= self.cnt.get('pe', 0), "dependency on a PE signal that is not emitted yet"
            if mark is not None and k == mark[0] and v >= mark[1]:
                continue
            ws.append((k, v))
        for c in rcells:
            d = self.R.setdefault(c, {})
            d[mark[0]] = max(d.get(mark[0], 0), mark[1])
        for c in wcells:
            if eng == 'pe':
                d = self.W.setdefault(c, {})
                d[mark[0]] = max(d.get(mark[0], 0), mark[1])
                self.R[c] = {k: v for k, v in self.R.get(c, {}).items() if k == 'pe'}
            else:
                self.W[c] = {mark[0]: mark[1]}
                self.R[c] = {}
        self.ops[eng].append((fn, ws, tok, dma_sem is not None))
        return tok


def build():
    nc = bass.Bass("TRN2", target_bir_lowering=False)
    OFF, TOTAL = plan()
    xo_d = nc.dram_tensor("x_own", [16, 128, TOK], F32, kind="ExternalInput").ap()
    xp_d = nc.dram_tensor("x_prev", [16, 128, TOK], F32, kind="ExternalInput").ap()
    mem_d = nc.dram_tensor("memT", [16, 128, 256], F32, kind="ExternalInput").ap()
    prm_d = nc.dram_tensor("prm", [128, NP], F32, kind="ExternalInput").ap()
    cst_d = nc.dram_tensor("cst", [128, 384], F32, kind="ExternalInput").ap()
    w_d = nc.dram_tensor("wbig", [128, TOTAL], F32, kind="ExternalInput").ap()
    out_d = nc.dram_tensor("out", [16, 128, TOK], F32, kind="ExternalOutput").ap()
    xs_d = nc.dram_tensor("xspill", [16, 128, TOK], F32, kind="Internal").ap()

    S = Sched()
    with ExitStack() as es:
        A = es.enter_context(nc.sbuf_tensor("arena", [128, 103 * 1024], BF16))
        PS = [es.enter_context(nc.psum_tensor(f"ps{i}", [128, 512], F32))[:, :] for i in range(8)]
        sem_names = ['pe', 'act', 'dve', 'pool', 'w0', 'w1', 'w2', 'w3', 'ldx0', 'ldx1', 'ldx2', 'ldx3', 'ldx4', 'ldx5', 'ldx6', 'ldx7', 'ldp', 'ldc', 'ldm', 'st', 'spill', 'rld']
        sems = {k: es.enter_context(nc.semaphore(k)) for k in sem_names}
        block = es.enter_context(nc.Block())

        o_XR, o_HR, o_HPR, o_ACTR, o_WS, o_MISC = 0, 32768, 49152, 65536, 76800, 93184

        def bf(o, n):
            return A[:, o:o + n]

        def fp(o, n):
            return A[:, o:o + 2 * n].bitcast(F32)

        X = fp(o_XR, 16384).rearrange("p (c t) -> p c t", c=16)
        H = bf(o_HR, 16384).rearrange("p (c t) -> p c t", c=16)
        HP = bf(o_HPR, 16384).rearrange("p (c t) -> p c t", c=16)
        MERGED = HP
        ACTB = bf(o_ACTR, GJ * 1024).rearrange("p (c t) -> p c t", c=GJ)
        WSL = [bf(o_WS + i * SLOT, SLOT) for i in range(NS)]
        m = o_MISC

        def al(n):
            return (n + 127) // 128 * 128

        CST = bf(m, 384); m += al(384)
        ONES, IDENT, MASKNEG = CST[:, 0:128], CST[:, 128:256], CST[:, 256:384]
        ONESF = bf(m, 128); m += al(128)
        PRM = fp(m, NP); m += al(2 * NP)
        RS = [fp(m + i * 1024, 512) for i in range(2)]; m += 2048
        SQ = [bf(m + i * 512, 512) for i in range(4)]; m += 2048
        o_SG = m
        SG = [fp(m + i * 1024, 512) for i in range(2)]; m += 2048
        SC = fp(m, 16); m += al(32)
        LAMP = fp(m, 256); m += al(512)
        assert m <= 103 * 1024 and o_SG + 4096 <= 103 * 1024
        YA = bf(o_XR, 16384).rearrange("p (c t) -> p c t", c=16)
        YC = bf(o_XR + 16384, 16384).rearrange("p (c t) -> p c t", c=16)
        o = o_XR + 16384
        Qb = bf(o, 2048).rearrange("p (c t) -> p c t", c=2); o += 2048
        KT = bf(o, 4096).rearrange("p (c t) -> p c t", c=2); o += 4096
        Vb = bf(o, 4096).rearrange("p (c t) -> p c t", c=16); o += 4096
        o = o_ACTR
        OD = fp(o, 2048).rearrange("p (c t) -> p c t", c=2); o += 4096
        T1 = fp(o, 1024).rearrange("p (c t) -> p c t", c=2); o += 2048
        R1 = fp(o, 512); o += 1024
        R2 = fp(o, 512); o += 1024
        PT = [bf(o + i * 512, 512) for i in range(4)]; o += 2048
        o = o_ACTR
        Ub = fp(o, 1024); o += 2048
        CUH = fp(o, 1028); o += 2176
        Ab = fp(o, 1024); o += 2048
        HAL = fp(o, 4); o += 128
        o = o_ACTR
        SGA = fp(o, 1024); o += 2048
        SGC = fp(o, 1024); o += 2048
        M1 = fp(o, 1024); o += 2048
        M2 = fp(o, 1024); o += 2048
        MT = fp(o_HR, 4096).rearrange("p (c t) -> p c t", c=16)
        MH = bf(o_SG, 4096).rearrange("p (c t) -> p c t", c=16)
        QX = bf(o_HPR + 12288, 4096).rearrange("p (c t) -> p c t", c=4)
        o = o_ACTR
        KX = bf(o, 1024).rearrange("p (c t) -> p c t", c=4); o += 1024
        VX = bf(o, 1024).rearrange("p (c t) -> p c t", c=2); o += 1024
        OX = bf(o, 4096).rearrange("p (c t) -> p c t", c=4); o += 4096
        PTX = [bf(o + i * 512, 512) for i in range(4)]; o += 2048
        RX = fp(o, 512); o += 1024

        wr_tok = {}
        rd_tok = {}
        for b_ in range(8):
            wr_tok[b_] = None
            rd_tok[b_] = []
            for h_ in range(2):
                wr_tok[(b_, h_)] = None
                rd_tok[(b_, h_)] = []

        def MM(bank, out, lhsT, rhs, start, stop, waits=(), sig=None, sgc=False):
            ws = list(waits)
            if start:
                if isinstance(bank, tuple):
                    ws.append(list(rd_tok[bank]))
                    ws.append(list(rd_tok[bank[0]]))
                    rd_tok[bank] = []
                else:
                    for k_ in (bank, (bank, 0), (bank, 1)):
                        ws.append(list(rd_tok[k_]))
                        rd_tok[k_] = []
            if sig is None:
                sig = stop
            if sgc:
                fn = lambda e, o=out, l=lhsT, r=rhs, s=start, p=stop: e.matmul(o, l, r, start=s, stop=p, skip_group_check=True)
            else:
                fn = lambda e, o=out, l=lhsT, r=rhs, s=start, p=stop: e.matmul(o, l, r, start=s, stop=p)
            tok = S.op('pe', fn, ws, sig)
            if stop:
                wr_tok[bank] = tok
            return tok

        def EV(eng, fn, banks=(), waits=()):
            ws = list(waits) + [wr_tok[b] for b in banks]
            tok = S.op(eng, fn, ws, sig=True)
            for b in banks:
                rd_tok[b].append(tok)
            return tok

        def act_copy(out, in_):
            return lambda e, o=out, i=in_: e.activation(out=o, in_=i, func=AF.Copy)

        def act_fn(out, in_, func, **kw):
            return lambda e, o=out, i=in_, f=func, k=kw: e.activation(out=o, in_=i, func=f, **k)

        def dve_copy(out, in_):
            return lambda e, o=out, i=in_: e.tensor_copy(out=o, in_=i)

        def dve_tt(out, a, b, op):
            return lambda e, o=out, x=a, y=b, p=op: e.tensor_tensor(out=o, in0=x, in1=y, op=p)

        def dve_ts(out, a, s1, s2, op0, op1=None):
            if op1 is None:
                return lambda e, o=out, x=a, s=s1, p=op0: e.tensor_scalar(out=o, in0=x, scalar1=s, scalar2=None, op0=p)
            return lambda e, o=out, x=a, s=s1, s_2=s2, p=op0, q=op1: e.tensor_scalar(out=o, in0=x, scalar1=s, scalar2=s_2, op0=p, op1=q)

        def dve_stt(out, a, s, b, op0, op1):
            return lambda e, o=out, x=a, sc=s, y=b, p=op0, q=op1: e.scalar_tensor_tensor(out=o, in0=x, scalar=sc, in1=y, op0=p, op1=q)

        def dve_recip(out, in_):
            return lambda e, o=out, i=in_: e.reciprocal(out=o, in_=i)

        def barrier(engs=('pe', 'act', 'dve')):
            if not _os.environ.get("K_BARRIER"):
                return
            toks = [S.last.get(e) for e in ('pe', 'act', 'dve')]
            for e in engs:
                S.op(e, None, toks)

        wstate = {'i': 0, 'rel': {}}

        def fetch(key):
            off, size = OFF[key]
            i = wstate['i']
            wstate['i'] += 1
            slot = i % NS
            waits = []
            if i >= NS:
                waits.append(wstate['rel'][i - NS])
            elif i >= 1 and 'x0' in wstate:
                waits.append(list(wstate['x0']))
            tok = S.op('pool', lambda e, s=slot, o=off, n=size: e.dma_start(out=WSL[s][:, 0:n], in_=w_d[:, o:o + n]),
                       waits, dma_sem=f'w{slot}', wr=[WSL[slot][:, 0:size]])
            return i, WSL[slot], tok

        def release(i, tok):
            wstate['rel'][i] = tok

        def tsl(t):
            return slice(t * T, (t + 1) * T)

        t_prm = S.op('sp', lambda e: e.dma_start(out=PRM, in_=prm_d), dma_sem='ldp', wr=[PRM])
        t_cst = S.op('pool', lambda e: e.dma_start(out=CST, in_=cst_d), dma_sem='ldc', wr=[CST])
        tk = S.op('dve', dve_ts(ONESF, ONES, PRM[:, P_FLAG:P_FLAG + 1], None, ALU.mult), [t_prm, t_cst], sig=True)
        LAMV = PRM[:, P_LAM:P_LAM + 512].rearrange("p (c t) -> p c t", c=4)
        LP = LAMP.rearrange("p (c t) -> p c t", c=2)
        tk = S.op('dve', dve_tt(LP[:, 0, :], LAMV[:, 0, :], LAMV[:, 1, :], ALU.mult), [t_prm], sig=True)
        tk = S.op('dve', dve_tt(LP[:, 1, :], LAMV[:, 2, :], LAMV[:, 3, :], ALU.mult), [tk], sig=True)
        tk = S.op('dve', lambda e: e.reduce_sum(out=SC[:, 0:1], in_=LP[:, 0, :], axis=mybir.AxisListType.X), [tk], sig=True, rd=[LAMP], wr=[SC])
        tk = S.op('dve', lambda e: e.reduce_sum(out=SC[:, 1:2], in_=LP[:, 1, :], axis=mybir.AxisListType.X), [tk], sig=True, rd=[LAMP], wr=[SC])
        tk = S.op('act', act_fn(SC[:, 2:4], SC[:, 0:2], AF.Exp), [tk], sig=True)
        tk = S.op('dve', dve_tt(SC[:, 4:5], SC[:, 2:3], SC[:, 3:4], ALU.subtract), [tk], sig=True)
        tk = S.op('dve', dve_ts(SC[:, 5:6], SC[:, 4:5], 0.2, -1.0, ALU.add, ALU.mult), [tk], sig=True)
        NEGLAM = SC[:, 5:6]
        t_setup = tk

        def rmsnorm(src, nchunk, ntile, tw, gcol, dst, eps, ndim, waits, stat_banks=(6, 7), post_scale=None, cwaits=None, ctoks=None):
            first = True
            last = None
            tile_toks = []
            for t in range(ntile):
                sl = slice(t * tw, (t + 1) * tw)
                bank = stat_banks[t % 2]
                for c in range(nchunk):
                    w = [waits] if first else []
                    first = False
                    if cwaits is not None and t == 0:
                        w.append(cwaits[c])
                    k = c % 4
                    tsq = S.op('act', act_fn(SQ[k][:, 0:tw], src[:, c, sl], AF.Square), w + [sq_rd[k]], sig=True)
                    sq_rd[k] = MM(bank, PS[bank][:, 0:tw], ONES, SQ[k][:, 0:tw], c == 0, c == nchunk - 1, [tsq], sig=True)
                r = RS[t % 2][:, 0:tw]
                t1 = EV('dve', dve_ts(r, PS[bank][:, 0:tw], 1.0 / ndim, eps, ALU.mult, ALU.add), [bank], [rs_rd[t % 2]])
                t2 = S.op('act', act_fn(r, r, AF.Sqrt), [t1], sig=True)
                t3 = S.op('dve', dve_recip(r, r), [t2], sig=True)
                if post_scale is not None:
                    t3 = S.op('dve', dve_ts(r, r, post_scale, None, ALU.mult), [t3], sig=True)
                for c in range(nchunk):
                    last = S.op('dve', dve_stt(dst[:, c, sl], src[:, c, sl], PRM[:, gcol + c:gcol + c + 1], r, ALU.mult, ALU.mult),
                                [t3], sig=True)
                    if ctoks is not None:
                        ctoks[c] = last
                rs_rd[t % 2] = last
                tile_toks.append(last)
            return tile_toks

        sq_rd = [None] * 4
        rs_rd = [None] * 2

        def ffn(f, h_ready):
            sg_rd = [None, None]
            last_down_pe = None
            x_tok = None
            for g in range(NG):
                nj = min(GJ, NJ - GJ * g)
                act_tok = None
                for jj in range(nj):
                    j = g * GJ + jj
                    bi, blk, ltok = fetch(('gu', f, j))
                    bv = blk.rearrange("p (a k m) -> p a k m", a=2, k=16)
                    tg = [None, None]
                    tu = [None, None]
                    if j == 0:
                        for t in range(2):
                            for kc in range(16):
                                tg[t] = MM(t, PS[t], bv[:, 0, kc, :], H[:, kc, tsl(t)], kc == 0, kc == 15,
                                           [ltok, h_ready[t]] if kc == 0 else [])
                            for kc in range(16):
                                tu[t] = MM(2 + t, PS[2 + t], bv[:, 1, kc, :], H[:, kc, tsl(t)], kc == 0, kc == 15)
                    else:
                        for kc in range(16):
                            for t in range(2):
                                tg[t] = MM(t, PS[t], bv[:, 0, kc, :], H[:, kc, tsl(t)], kc == 0, kc == 15,
                                           [ltok] if kc == 0 else [])
                        for kc in range(16):
                            for t in range(2):
                                tu[t] = MM(2 + t, PS[2 + t], bv[:, 1, kc, :], H[:, kc, tsl(t)], kc == 0, kc == 15)
                    release(bi, tu[1])
                    for t in range(2):
                        ts_ = EV('act', act_fn(SG[t], PS[t], AF.Silu), [t], [sg_rd[t]])
                        act_tok = EV('dve', dve_tt(ACTB[:, jj, tsl(t)], PS[2 + t], SG[t], ALU.mult), [2 + t],
                                     [ts_, last_down_pe])
                        sg_rd[t] = act_tok
                for c in range(16):
                    bi, blk, ltok = fetch(('dn', f, g, c))
                    bv = blk[:, 0:nj * 128].rearrange("p (j m) -> p j m", j=nj)
                    tks = [None, None]
                    for jj in range(nj):
                        for t in range(2):
                            bank = 4 + 2 * (c % 2) + t
                            tks[t] = MM(bank, PS[bank], bv[:, jj, :], ACTB[:, jj, tsl(t)], jj == 0, jj == nj - 1,
                                        [ltok] if jj == 0 else [])
                    release(bi, tks[1])
                    last_down_pe = tks[1]
                    for t in range(2):
                        bank = 4 + 2 * (c % 2) + t
                        x_tok = EV('dve', dve_stt(X[:, c, tsl(t)], PS[bank], 0.5, X[:, c, tsl(t)], ALU.mult, ALU.add), [bank])
            return x_tok

        def load_x(src_d):
            i = 0
            toks = []
            for t in range(2):
                for g4 in range(4):
                    cs = slice(4 * g4, 4 * g4 + 4)
                    toks.append(S.op('sp', lambda e, cs=cs, t=t: e.dma_start(out=X[:, cs, tsl(t)], in_=src_d[cs, :, tsl(t)].rearrange("c p t -> p c t")),
                                     [], dma_sem=f'ldx{i}', wr=[X[:, cs, tsl(t)]]))
                    i += 1
            return toks

        wstate['x0'] = load_x(xp_d)[0:4]
        th = rmsnorm(X, 16, 2, T, P_FFN1, H, 1e-6, D, [t_setup])
        tx1 = ffn(1, th)
        barrier()
        thp = rmsnorm(X, 16, 2, T, P_MIX, HP, 1e-6, D, [])
        load_x(xo_d)
        th = rmsnorm(X, 16, 2, T, P_FFN1, H, 1e-6, D, [])
        tx1 = ffn(1, th)
        barrier()
        th = rmsnorm(X, 16, 2, T, P_MIX, H, 1e-6, D, [])
        barrier()
        t_spill = []
        for i_, c0_ in enumerate((8, 10, 12, 14, 0, 2, 4, 6)):
            cs_ = slice(c0_, c0_ + 2)
            t_spill.append(S.op('sp', lambda e, cs=cs_: e.dma_start(out=xs_d[cs].rearrange("c p t -> p c t"), in_=X[:, cs, :]),
                                [], dma_sem=f'ldx{i_}', rd=[X[:, cs_, :]]))

        TA = 256
        PT8 = [PT[i // 2][:, (i % 2) * 256:(i % 2 + 1) * 256] for i in range(8)]

        def head_proj_q(hd):
            bi, blk, ltok = fetch(('q', hd))
            bv = blk.rearrange("p (k m) -> p k m", k=16)
            if hd == 0:
                tk = None
                for t in range(2):
                    for comp in range(2):
                        bank = 2 * comp + t
                        for kc in range(16):
                            tk = MM(bank, PS[bank], bv[:, kc, comp * 128:(comp + 1) * 128], H[:, kc, tsl(t)],
                                    kc == 0, kc == 15, [ltok] if kc == 0 else [])
                        EV('act', act_copy(Qb[:, comp, tsl(t)], PS[bank]), [bank])
                release(bi, tk)
                return
            for comp in range(2):
                banks = (0, 1) if comp == 0 else (2, 3)
                tk_ = [None, None]
                for kc in range(16):
                    for t in range(2):
                        tk_[t] = MM(banks[t], PS[banks[t]], bv[:, kc, comp * 128:(comp + 1) * 128], H[:, kc, tsl(t)],
                                    kc == 0, kc == 15, [ltok] if kc == 0 else [])
                if comp == 1:
                    release(bi, tk_[1])
                for t in range(2):
                    EV('act', act_copy(Qb[:, comp, tsl(t)], PS[banks[t]]), [banks[t]])

        def head_proj_kv(hd):
            bi, blk, ltok = fetch(('k', hd))
            bv = blk.rearrange("p (k m) -> p k m", k=16)
            for comp in range(2):
                banks = (0, 1, 5, 6) if comp == 0 else (2, 3, 4, 7)
                tk_ = [None] * 4
                for kc in range(16):
                    for tt in range(4):
                        src = HP if tt < 2 else H
                        tk_[tt] = MM(banks[tt], PS[banks[tt]], bv[:, kc, comp * 128:(comp + 1) * 128], src[:, kc, tsl(tt % 2)],
                                     kc == 0, kc == 15, [ltok] if kc == 0 else [])
                if comp == 1:
                    release(bi, tk_[3])
                for tt in range(4):
                    eng = 'dve' if tt % 2 == 0 else 'act'
                    fn = dve_copy(KT[:, comp, tsl(tt)], PS[banks[tt]]) if eng == 'dve' else act_copy(KT[:, comp, tsl(tt)], PS[banks[tt]])
                    EV(eng, fn, [banks[tt]])
            bi, blk, ltok = fetch(('v', hd))
            bv = blk.rearrange("p (k m) -> p k m", k=16)
            tk_ = None
            for tc in range(16):
                src = HP if tc < 8 else H
                tcl = tc % 8
                bank = 4 + (tc % 4)
                for kc in range(16):
                    tk_ = MM(bank, PS[bank][:, 0:256], src[:, kc, tcl * 128:(tcl + 1) * 128], bv[:, kc, :],
                             kc == 0, kc == 15, [ltok] if kc == 0 else [])
                if tc < 8:
                    EV('dve', dve_ts(Vb[:, tc, :], PS[bank][:, 0:256], PRM[:, P_FLAG:P_FLAG + 1], None, ALU.mult), [bank])
                else:
                    EV('act', act_copy(Vb[:, tc, :], PS[bank][:, 0:256]), [bank])
            release(bi, tk_)
            return [S.last['act'], S.last['dve']]

        def head_attn(hd, q_ready):
            steps = []
            for qt in range(4):
                for comp in range(2):
                    lst = [(kc, 0, False, True) for kc in range(8)]
                    for oc in range(2 * qt + 2):
                        if oc < 2 * qt:
                            lst.append((8 + oc, 0, False, False))
                        else:
                            lst.append((8 + oc, (oc - 2 * qt) * 128, True, False))
                    for i, (kcg, col0, diag, prev) in enumerate(lst):
                        steps.append(dict(qt=qt, comp=comp, kcg=kcg, col0=col0, diag=diag, prev=prev,
                                          first=(i == 0), last=(i == len(lst) - 1)))
            n = len(steps)
            exp_tok = [None] * n
            pv_tok = [None] * n
            PD = 3

            def sview(s, a, b_):
                bank = s % 4
                return bank, PS[bank][:, a:b_]

            def qk(s):
                st = steps[s]
                c0 = st['col0']
                key, out = sview(s, c0, 256)
                w = []
                tk_ = MM(key, out, KT[:, st['comp'], st['kcg'] * 128:(st['kcg'] + 1) * 128],
                         Qb[:, st['comp'], st['qt'] * TA + c0:(st['qt'] + 1) * TA], True, not st['diag'], w)
                if st['diag']:
                    key, out = sview(s, c0, c0 + 128)
                    tk_ = MM(key, out, IDENT, MASKNEG, False, True)
                return tk_

            def ex(s):
                st = steps[s]
                c0 = st['col0']
                key, src = sview(s, c0, 256)
                w = [pv_tok[s - 8]] if s >= 8 else []
                exp_tok[s] = EV('act', act_fn(PT8[s % 8][:, c0:256], src, AF.Exp, scale=SCALE), [key], w)

            def pv(s):
                st = steps[s]
                c0 = st['col0']
                comp = st['comp']
                qt = st['qt']
                ob = 4 if comp == 0 else 6
                lb = 5 if comp == 0 else 7
                p = PT8[s % 8][:, c0:256]
                for dvc in range(2):
                    MM(ob, PS[ob][:, dvc * 256 + c0:(dvc + 1) * 256], Vb[:, st['kcg'], dvc * 128:(dvc + 1) * 128], p,
                       st['first'] and dvc == 0, st['last'], [exp_tok[s]] if dvc == 0 else [], sgc=True)
                pv_tok[s] = MM(lb, PS[lb][:, c0:256], ONESF if st['prev'] else ONES, p, st['first'], st['last'], sig=True)
                if st['last']:
                    qs = slice(qt * TA, (qt + 1) * TA)
                    if comp == 0:
                        a = EV('dve', dve_recip(R1[:, 0:256], PS[5][:, 0:256]), [5])
                        for dvc in range(2):
                            EV('dve', dve_tt(T1[:, dvc, 0:256], PS[4][:, dvc * 256:(dvc + 1) * 256], R1[:, 0:256], ALU.mult), [4], [a])
                    else:
                        a = EV('dve', dve_recip(R2[:, 0:256], PS[7][:, 0:256]), [7])
                        a = S.op('dve', dve_ts(R2[:, 0:256], R2[:, 0:256], NEGLAM, None, ALU.mult), [a], sig=True)
                        for dvc in range(2):
                            b = EV('dve', dve_tt(OD[:, dvc, qs], PS[6][:, dvc * 256:(dvc + 1) * 256], R2[:, 0:256], ALU.mult), [6], [a])
                            S.op('dve', dve_tt(OD[:, dvc, qs], OD[:, dvc, qs], T1[:, dvc, 0:256], ALU.add), [b], sig=True)

            for s in range(min(PD, n)):
                qk(s)
                ex(s)
            for s in range(n):
                if s + PD < n:
                    qk(s + PD)
                    ex(s + PD)
                pv(s)
            return S.last['dve']

        def head_subln_sq(od_tok):
            toks = {}
            for t in range(2):
                for dvc in range(2):
                    k = 2 * t + dvc
                    toks[(t, dvc)] = S.op('act', act_fn(SQ[k], OD[:, dvc, tsl(t)], AF.Square), [od_tok, sq_rd[k]], sig=True)
            return toks

        def head_subln(hd, sqt):
            for t in range(2):
                bank = 4 if t == 0 else 7
                for dvc in range(2):
                    k = 2 * t + dvc
                    sq_rd[k] = MM(bank, PS[bank], ONES, SQ[k], dvc == 0, dvc == 1, [sqt[(t, dvc)]], sig=True)
                r = RS[t]
                t1 = EV('dve', dve_ts(r, PS[bank], 1.0 / 256, 1e-5, ALU.mult, ALU.add), [bank], [rs_rd[t]])
                t2 = S.op('act', act_fn(r, r, AF.Sqrt), [t1], sig=True)
                t3 = S.op('dve', dve_recip(r, r), [t2], sig=True)
                t3 = S.op('dve', dve_ts(r, r, 0.8, None, ALU.mult), [t3], sig=True)
                for dvc in range(2):
                    rs_rd[t] = S.op('dve', dve_stt(YA[:, 2 * hd + dvc, tsl(t)], OD[:, dvc, tsl(t)],
                                                   PRM[:, P_SUB + dvc:P_SUB + dvc + 1], r, ALU.mult, ALU.mult), [t3], sig=True)

        sqt = None
        for hd in range(8):
            head_proj_q(hd)
            if hd > 0:
                head_subln(hd - 1, sqt)
            qr = head_proj_kv(hd)
            od_tok = head_attn(hd, qr)
            sqt = head_subln_sq(od_tok)
        head_subln(7, sqt)
        barrier(engs=('act', 'dve'))

        CU = CUH[:, 2:1026]
        for c in range(16):
            tkc = {}
            for nm, b0, hb in (('cC', 0, 6), ('cU', 2, 7), ('cB', 4, None)):
                bi, blk, ltok = fetch((nm, c))
                bv = blk[:, 0:2048].rearrange("p (k m) -> p k m", k=16)
                tk_ = [None, None]
                for kc in range(16):
                    for t in range(2):
                        tk_[t] = MM(b0 + t, PS[b0 + t], bv[:, kc, :], H[:, kc, tsl(t)], kc == 0, kc == 15,
                                    [ltok] if kc == 0 else [])
                if hb is not None:
                    for kc in range(16):
                        tk_[1] = MM(hb, PS[hb][:, 0:2], bv[:, kc, :], HP[:, kc, 1022:1024], kc == 0, kc == 15)
                release(bi, tk_[1])
            tu_ = None
            for t in range(2):
                tu_ = EV('act', act_copy(Ub[:, tsl(t)], PS[2 + t]), [2 + t], [conv_rd] if c > 0 and t == 0 else [])
            th_ = EV('act', act_copy(HAL[:, 0:2], PS[7][:, 0:2]), [7])
            a = None
            for t in range(2):
                a = EV('dve', dve_tt(CU[:, tsl(t)], PS[t], Ub[:, tsl(t)], ALU.mult), [t], [tu_])
            a = EV('dve', dve_tt(CUH[:, 0:2], PS[6][:, 0:2], HAL[:, 0:2], ALU.mult), [6], [th_, a])
            a = S.op('dve', dve_ts(CUH[:, 0:2], CUH[:, 0:2], PRM[:, P_FLAG:P_FLAG + 1], None, ALU.mult), [a], sig=True)
            w0 = PRM[:, P_CW + c:P_CW + c + 1]
            w1 = PRM[:, P_CW + 16 + c:P_CW + 16 + c + 1]
            w2 = PRM[:, P_CW + 32 + c:P_CW + 32 + c + 1]
            a = S.op('dve', dve_ts(Ab, CUH[:, 0:1024], w0, None, ALU.mult), [a], sig=True)
            a = S.op('dve', dve_stt(Ab, CUH[:, 1:1025], w1, Ab, ALU.mult, ALU.add), [a], sig=True)
            a = S.op('dve', dve_stt(Ab, CUH[:, 2:1026], w2, Ab, ALU.mult, ALU.add), [a], sig=True)
            for t in range(2):
                a = EV('dve', dve_tt(YC[:, c, tsl(t)], PS[4 + t], Ab[:, tsl(t)], ALU.mult), [4 + t], [a])
            conv_rd = a
        barrier()

        mrd = None
        for c in range(16):
            specs = (('ga', H, 0), ('gc', H, 2), ('ao', YA, 4), ('co', YC, 6))
            for nm, src, b0 in specs:
                bi, blk, ltok = fetch((nm, c))
                bv = blk[:, 0:2048].rearrange("p (k m) -> p k m", k=16)
                tk_ = [None, None]
                for kc in range(16):
                    for t in range(2):
                        tk_[t] = MM(b0 + t, PS[b0 + t], bv[:, kc, :], src[:, kc, tsl(t)], kc == 0, kc == 15,
                                    [ltok] if kc == 0 else [])
                release(bi, tk_[1])
            s1 = s2 = None
            for t in range(2):
                s1 = EV('act', act_fn(SGA[:, tsl(t)], PS[0 + t], AF.Sigmoid, bias=PRM[:, P_BGA + c:P_BGA + c + 1]), [0 + t],
                        [mrd] if t == 0 else [])
                s2 = EV('act', act_fn(SGC[:, tsl(t)], PS[2 + t], AF.Sigmoid, bias=PRM[:, P_BGC + c:P_BGC + c + 1]), [2 + t])
            for t in range(2):
                a = EV('dve', dve_tt(M1[:, tsl(t)], PS[4 + t], SGA[:, tsl(t)], ALU.mult), [4 + t], [s1, s2])
                b = EV('dve', dve_tt(M2[:, tsl(t)], PS[6 + t], SGC[:, tsl(t)], ALU.mult), [6 + t], [a])
                mrd = S.op('dve', dve_tt(MERGED[:, c, tsl(t)], M1[:, tsl(t)], M2[:, tsl(t)], ALU.add), [b], sig=True)
        barrier()
        t_rld = [S.op('sp', lambda e, q=q: e.dma_start(out=X[:, 4 * q:4 * q + 4, :], in_=xs_d[4 * q:4 * q + 4].rearrange("c p t -> p c t")),
                      [S.last['pe'], S.last['dve'], S.last['act'], t_spill], dma_sem=f'ldx{q}', wr=[X[:, 4 * q:4 * q + 4, :]]) for q in range(4)]
        t_mem = S.op('sp', lambda e: e.dma_start(out=MT, in_=mem_d.rearrange("c p t -> p c t")), [], dma_sem='ldm', wr=[MT])
        xt = None
        for c in range(16):
            if c == 8:
                tmh = rmsnorm(MT, 16, 1, 256, P_MEM, MH, 1e-6, D, [t_mem])
            bi, blk, ltok = fetch(('mo', c))
            bv = blk[:, 0:2048].rearrange("p (k m) -> p k m", k=16)
            tk_ = [None, None]
            b0 = 2 * (c % 4)
            for kc in range(16):
                for t in range(2):
                    tk_[t] = MM(b0 + t, PS[b0 + t], bv[:, kc, :], MERGED[:, kc, tsl(t)], kc == 0, kc == 15,
                                [ltok] if kc == 0 else [])
            release(bi, tk_[1])
            for t in range(2):
                xt = EV('dve', dve_tt(X[:, c, tsl(t)], PS[b0 + t], X[:, c, tsl(t)], ALU.add), [b0 + t], [t_rld[c // 4]])
        barrier()

        th = rmsnorm(X, 16, 2, T, P_XA, H, 1e-6, D, [])
        for hd in range(4):
            bi, blk, ltok = fetch(('xk', hd))
            bv = blk[:, 0:2048].rearrange("p (k m) -> p k m", k=16)
            tk_ = None
            bank = hd % 2
            for kc in range(16):
                tk_ = MM(bank, PS[bank][:, 0:256], bv[:, kc, :], MH[:, kc, :], kc == 0, kc == 15, [ltok] if kc == 0 else [])
            release(bi, tk_)
            EV('act', act_copy(KX[:, hd, :], PS[bank][:, 0:256]), [bank])
        for i in range(2):
            bi, blk, ltok = fetch(('xv', i))
            bv = blk.rearrange("p (k m) -> p k m", k=16)
            tk_ = None
            for mc in range(2):
                bank = 2 + mc
                for kc in range(16):
                    tk_ = MM(bank, PS[bank][:, 0:256], MH[:, kc, mc * 128:(mc + 1) * 128], bv[:, kc, :], kc == 0, kc == 15,
                             [ltok] if kc == 0 else [])
                EV('dve', dve_copy(VX[:, mc, i * 256:(i + 1) * 256], PS[bank][:, 0:256]), [bank])
            release(bi, tk_)
        qb = [fetch(('xq', hd)) for hd in range(4)]
        for t in range(2):
            for hd in range(4):
                bi, blk, ltok = qb[hd]
                bv = blk[:, 0:2048].rearrange("p (k m) -> p k m", k=16)
                bank = 4 + hd
                tk = None
                for kc in range(16):
                    tk = MM(bank, PS[bank], bv[:, kc, :], H[:, kc, tsl(t)], kc == 0, kc == 15, [ltok] if kc == 0 else [])
                if t == 1:
                    release(bi, tk)
                EV('act' if hd % 2 == 0 else 'dve',
                   act_copy(QX[:, hd, tsl(t)], PS[bank]) if hd % 2 == 0 else dve_copy(QX[:, hd, tsl(t)], PS[bank]), [bank])
        barrier()
        xsteps = [(hd, t, mc) for hd in range(4) for t in range(2) for mc in range(2)]
        nx = len(xsteps)
        x_exp = [None] * nx
        x_pv = [None] * nx

        def x_qk(si):
            hd, t, mc = xsteps[si]
            sb = si % 4
            MM(sb, PS[sb], KX[:, hd, mc * 128:(mc + 1) * 128], QX[:, hd, tsl(t)], True, True)
            x_exp[si] = EV('act', act_fn(PTX[si % 4], PS[sb], AF.Exp, scale=SCALE), [sb],
                           [x_pv[si - 4]] if si >= 4 else [])

        def x_pvf(si):
            hd, t, mc = xsteps[si]
            ob = 4 + ((si // 2) % 2) * 2
            p = PTX[si % 4]
            MM(ob, PS[ob], VX[:, mc, hd * 128:(hd + 1) * 128], p, mc == 0, mc == 1, [x_exp[si]])
            x_pv[si] = MM(ob + 1, PS[ob + 1], ONES, p, mc == 0, mc == 1, sig=True)
            if mc == 1:
                a = EV('dve', dve_recip(RX, PS[ob + 1]), [ob + 1])
                EV('dve', dve_tt(OX[:, hd, tsl(t)], PS[ob], RX, ALU.mult), [ob], [a])

        XPD = 2
        for si in range(min(XPD, nx)):
            x_qk(si)
        for si in range(nx):
            if si + XPD < nx:
                x_qk(si + XPD)
            x_pvf(si)
        blks = [fetch(('xo', i)) for i in range(2)]
        for c in range(16):
            bi, blk, ltok = blks[c // 8]
            bv = blk.rearrange("p (k m) -> p k m", k=4)
            b0 = 2 * (c % 4)
            tk_ = [None, None]
            for kc in range(4):
                for t in range(2):
                    tk_[t] = MM(b0 + t, PS[b0 + t], bv[:, kc, (c % 8) * 128:(c % 8 + 1) * 128], OX[:, kc, tsl(t)], kc == 0, kc == 3,
                                [ltok] if kc == 0 else [])
            if c % 8 == 7:
                release(bi, tk_[1])
            for t in range(2):
                xt = EV('dve', dve_tt(X[:, c, tsl(t)], PS[b0 + t], X[:, c, tsl(t)], ALU.add), [b0 + t])
        barrier()

        th = rmsnorm(X, 16, 2, T, P_FFN2, H, 1e-6, D, [])
        tx2 = ffn(2, th)
        barrier()
        tf = rmsnorm(X, 16, 2, T, P_FIN, X, 1e-6, D, [])
        t_out = None
        for t in range(2):
            for g4 in range(4):
                cs = slice(4 * g4, 4 * g4 + 4)
                t_out = S.op('sp', lambda e, cs=cs, t=t: e.dma_start(out=out_d[cs, :, tsl(t)].rearrange("c p t -> p c t"), in_=X[:, cs, tsl(t)]),
                             [], dma_sem='st', rd=[X[:, cs, tsl(t)]])
        S.op('sp', None, [t_out])

        def replay(name, eng):
            seen = {}
            for fn, waits, tok, is_dma in S.ops[name]:
                for (k, v) in waits:
                    if seen.get(k, 0) >= v:
                        continue
                    eng.wait_ge(sems[k], v)
                    seen[k] = v
                if fn is None:
                    continue
                ins = fn(eng)
                if tok is not None:
                    ins.then_inc(sems[tok[0]], 16 if is_dma else 1)

        @block.tensor
        def _(e):
            replay('pe', e)

        @block.scalar
        def _(e):
            replay('act', e)

        @block.vector
        def _(e):
            replay('dve', e)

        @block.gpsimd
        def _(e):
            replay('pool', e)

        @block.sync
        def _(e):
            replay('sp', e)
    return nc


def _blk_kn(W, n0, n1):
    K = W.shape[0]
    kc = K // 128
    n = n1 - n0
    return np.ascontiguousarray(W[:, n0:n1].reshape(kc, 128, n).transpose(1, 0, 2)).reshape(128, kc * n)


def _host_weights(inp):
    OFF, TOTAL = plan()
    wb = np.empty((128, TOTAL), np.float32)

    def put(key, arr):
        o, s = OFF[key]
        assert arr.shape == (128, s), (key, arr.shape, s)
        wb[:, o:o + s] = arr

    for f in (1, 2):
        Wg, Wu, Wd = inp[f'ffn{f}_w_gate'][0], inp[f'ffn{f}_w_up'][0], inp[f'ffn{f}_w_down'][0]
        for j in range(NJ):
            put(('gu', f, j), np.concatenate([_blk_kn(Wg, j * 128, (j + 1) * 128), _blk_kn(Wu, j * 128, (j + 1) * 128)], axis=1))
        for g in range(NG):
            nj = min(GJ, NJ - GJ * g)
            rows = Wd[g * GJ * 128:(g * GJ + nj) * 128]
            for c in range(16):
                put(('dn', f, g, c), _blk_kn(rows, c * 128, (c + 1) * 128))
    Wmi = inp['w_mix_in'][0]
    for hd in range(8):
        put(('q', hd), _blk_kn(Wmi, hd * 256, hd * 256 + 256))
        put(('k', hd), _blk_kn(Wmi, 2048 + hd * 256, 2048 + hd * 256 + 256))
        put(('v', hd), _blk_kn(Wmi, 4096 + hd * 256, 4096 + hd * 256 + 256))
    for c in range(16):
        for nm, base in (('cB', 6144), ('cC', 8192), ('cU', 10240), ('ga', 12288), ('gc', 14336)):
            put((nm, c), _blk_kn(Wmi, base + c * 128, base + (c + 1) * 128))
        put(('ao', c), _blk_kn(inp['w_attn_out'][0], c * 128, (c + 1) * 128))
        put(('co', c), _blk_kn(inp['w_conv_out'][0], c * 128, (c + 1) * 128))
        put(('mo', c), _blk_kn(inp['w_mix_out'][0], c * 128, (c + 1) * 128))
    for hd in range(4):
        put(('xq', hd), _blk_kn(inp['w_xq'][0], hd * 128, (hd + 1) * 128))
        put(('xk', hd), _blk_kn(inp['w_xkv'][0], hd * 128, (hd + 1) * 128))
    for i in range(2):
        put(('xv', i), _blk_kn(inp['w_xkv'][0], 512 + i * 256, 512 + (i + 1) * 256))
        put(('xo', i), _blk_kn(inp['w_xo'][0], i * 1024, (i + 1) * 1024))
    return wb


def _colmajor16(v):
    return np.ascontiguousarray(np.asarray(v, np.float32).reshape(16, 128).T)


def kernel(**inp):
    inp = {k: np.asarray(v) for k, v in inp.items()}
    x = inp['x'].astype(np.float32, copy=False)
    mem = inp['mem'].astype(np.float32, copy=False)
    wb = _host_weights(inp)
    prm = np.zeros((128, NP), np.float32)
    prm[:, P_FFN1:P_FFN1 + 16] = _colmajor16(inp['ffn1_norm'][0])
    prm[:, P_MIX:P_MIX + 16] = _colmajor16(inp['mix_norm'][0])
    prm[:, P_XA:P_XA + 16] = _colmajor16(inp['xattn_norm'][0])
    prm[:, P_MEM:P_MEM + 16] = _colmajor16(inp['mem_norm'][0])
    prm[:, P_FFN2:P_FFN2 + 16] = _colmajor16(inp['ffn2_norm'][0])
    prm[:, P_FIN:P_FIN + 16] = _colmajor16(inp['final_norm'])
    prm[:, P_BGA:P_BGA + 16] = _colmajor16(inp['b_gates'][0, 0])
    prm[:, P_BGC:P_BGC + 16] = _colmajor16(inp['b_gates'][0, 1])
    for j in range(3):
        prm[:, P_CW + 16 * j:P_CW + 16 * j + 16] = _colmajor16(inp['conv_w'][0, j])
    prm[:, P_SUB:P_SUB + 2] = np.asarray(inp['diff_subln'][0], np.float32).reshape(2, 128).T
    for i, nm in enumerate(('lambda_q1', 'lambda_k1', 'lambda_q2', 'lambda_k2')):
        prm[:, P_LAM + 128 * i:P_LAM + 128 * (i + 1)] = np.asarray(inp[nm][0], np.float32)[None, :]
    cst = np.zeros((128, 384), np.float32)
    cst[:, 0:128] = 1.0
    cst[:, 128:256] = np.eye(128, dtype=np.float32)
    kk = np.arange(128)[:, None]
    qq = np.arange(128)[None, :]
    cst[:, 256:384] = np.where(kk > qq, -30000.0, 0.0)

    def tr(a):
        return np.ascontiguousarray(a.T).reshape(16, 128, a.shape[0])

    in_maps = []
    for core in range(8):
        b, half = core // 2, core % 2
        p = prm.copy()
        p[:, P_FLAG] = float(half)
        in_maps.append({
            "x_own": tr(x[b, half * TOK:(half + 1) * TOK]),
            "x_prev": tr(x[b, 0:TOK]) if half == 1 else np.zeros((16, 128, TOK), np.float32),
            "memT": tr(mem[b]),
            "prm": p,
            "cst": cst,
            "wbig": wb,
        })
    nc = build()
    res = run_bass_kernel_spmd(nc, in_maps, core_ids=list(range(8)))
    out = np.empty((4, 2048, D), np.float32)
    for core in range(8):
        b, half = core // 2, core % 2
        o = np.asarray(res.results[core]["out"]).reshape(D, TOK)
        out[b, half * TOK:(half + 1) * TOK, :] = o.T
    return out
```

```python
import numpy as np
import os as _os
from contextlib import ExitStack
import concourse.bass as bass
import concourse.mybir as mybir
from concourse.bass_utils import run_bass_kernel_spmd

F32 = mybir.dt.float32
BF16 = mybir.dt.bfloat16
AF = mybir.ActivationFunctionType
ALU = mybir.AluOpType

D = 2048
DFF = 5504
NJ = 43
GJ = 11
NG = 4
TOK = 1024
T = 512
NS = 4
SLOT = 4096
SCALE = 128 ** -0.5

P_FFN1, P_MIX, P_XA, P_MEM, P_FFN2, P_FIN = 0, 16, 32, 48, 64, 80
P_BGA, P_BGC = 96, 112
P_CW = 128
P_SUB = 176
P_FLAG = 178
P_LAM = 180
NP = P_LAM + 512


def plan():
    keys = []
    for f in (1, 2):
        for j in range(NJ):
            keys.append((('gu', f, j), 4096))
        for g in range(NG):
            nj = min(GJ, NJ - GJ * g)
            for c in range(16):
                keys.append((('dn', f, g, c), nj * 128))
    for hd in range(8):
        for nm in 'qkv':
            keys.append(((nm, hd), 4096))
    for c in range(16):
        for nm in ('cB', 'cC', 'cU', 'ga', 'gc', 'ao', 'co', 'mo'):
            keys.append(((nm, c), 2048))
    for hd in range(4):
        keys.append((('xq', hd), 2048))
        keys.append((('xk', hd), 2048))
    for i in range(2):
        keys.append((('xv', i), 4096))
        keys.append((('xo', i), 4096))
    off = {}
    o = 0
    for k, s in keys:
        off[k] = (o, s)
        o += s
    return off, o


def _is_ap(x):
    return hasattr(x, 'offset') and hasattr(x, 'space') and hasattr(x, 'ap')


class Sched:
    G = 256

    def __init__(self):
        self.ops = {e: [] for e in ('pe', 'act', 'dve', 'pool', 'sp')}
        self.cnt = {}
        self.last = {}
        self.W = {}
        self.R = {}

    def _cells(self, ap):
        if 'DRAM' in str(ap.space):
            return ()
        dsz = mybir.dt.size(ap.dtype)
        dims = [(st, n) for st, n in list(ap.ap)[1:] if n > 1]
        nm = ap.tensor.name
        run = 1
        while dims and dims[-1][0] == run:
            run *= dims[-1][1]
            dims.pop()
        outer = 1
        for _, n in dims:
            outer *= n
        starts = [ap.offset]
        if outer <= 256:
            for st, n in dims:
                starts = [b + i * st for b in starts for i in range(n)]
            ext = run
        else:
            ext = run
            for st, n in dims:
                ext += (n - 1) * abs(st)
        cells = set()
        for b in starts:
            lo = b * dsz
            hi = lo + ext * dsz
            cells.update(range(lo // self.G, (hi - 1) // self.G + 1))
        return [(nm, c) for c in sorted(cells)]

    def op(self, eng, fn, waits=(), sig=False, dma_sem=None, rd=(), wr=()):
        rds, wrs = list(rd), list(wr)
        if fn is not None and fn.__defaults__:
            first = True
            for dflt in fn.__defaults__:
                if _is_ap(dflt):
                    (wrs if first else rds).append(dflt)
                    first = False
                elif isinstance(dflt, dict):
                    rds.extend(v for v in dflt.values() if _is_ap(v))
        tok = None
        if dma_sem is not None:
            self.cnt[dma_sem] = self.cnt.get(dma_sem, 0) + 16
            tok = (dma_sem, self.cnt[dma_sem])
            mark = tok
        elif sig:
            self.cnt[eng] = self.cnt.get(eng, 0) + 1
            tok = (eng, self.cnt[eng])
            self.last[eng] = tok
            mark = tok
        else:
            mark = (eng, self.cnt.get(eng, 0) + 1)
            assert eng == 'pe' or fn is None, "non-signalling op on a non-PE engine"
        ws = []
        for w in waits:
            if w is None:
                continue
            if isinstance(w, list):
                ws.extend([x for x in w if x is not None])
            else:
                ws.append(w)
        auto = {}
        rcells = [c for a in rds for c in self._cells(a)]
        wcells = [c for a in wrs for c in self._cells(a)]
        for c in rcells:
            for k, v in self.W.get(c, {}).items():
                auto[k] = max(auto.get(k, 0), v)
        for c in wcells:
            for d in (self.W.get(c, {}), self.R.get(c, {})):
                for k, v in d.items():
                    auto[k] = max(auto.get(k, 0), v)
        for k, v in auto.items():
            if eng == 'pe' and k == 'pe':
                continue
            if _os.environ.get("K_NOSAME") and k == eng:
                continue
            if _os.environ.get("K_NOAUTO"):
                continue
            if k == 'pe':
                assert v <= self.cnt.get('pe', 0), "dependency on a PE signal that is not emitted yet"
            if mark is not None and k == mark[0] and v >= mark[1]:
                continue
            ws.append((k, v))
        for c in rcells:
            d = self.R.setdefault(c, {})
            d[mark[0]] = max(d.get(mark[0], 0), mark[1])
        for c in wcells:
            if eng == 'pe':
                d = self.W.setdefault(c, {})
                d[mark[0]] = max(d.get(mark[0], 0), mark[1])
                self.R[c] = {k: v for k, v in self.R.get(c, {}).items() if k == 'pe'}
            else:
                self.W[c] = {mark[0]: mark[1]}
                self.R[c] = {}
        self.ops[eng].append((fn, ws, tok, dma_sem is not None))
        return tok


def build():
    nc = bass.Bass("TRN2", target_bir_lowering=False)
    OFF, TOTAL = plan()
    xo_d = nc.dram_tensor("x_own", [16, 128, TOK], F32, kind="ExternalInput").ap()
    xp_d = nc.dram_tensor("x_prev", [16, 128, TOK], F32, kind="ExternalInput").ap()
    mem_d = nc.dram_tensor("memT", [16, 128, 256], F32, kind="ExternalInput").ap()
    prm_d = nc.dram_tensor("prm", [128, NP], F32, kind="ExternalInput").ap()
    cst_d = nc.dram_tensor("cst", [128, 384], F32, kind="ExternalInput").ap()
    w_d = nc.dram_tensor("wbig", [128, TOTAL], F32, kind="ExternalInput").ap()
    out_d = nc.dram_tensor("out", [16, 128, TOK], F32, kind="ExternalOutput").ap()
    xs_d = nc.dram_tensor("xspill", [16, 128, TOK], F32, kind="Internal").ap()

    S = Sched()
    with ExitStack() as es:
        A = es.enter_context(nc.sbuf_tensor("arena", [128, 103 * 1024], BF16))
        PS = [es.enter_context(nc.psum_tensor(f"ps{i}", [128, 512], F32))[:, :] for i in range(8)]
        sem_names = ['pe', 'act', 'dve', 'pool', 'w0', 'w1', 'w2', 'w3', 'ldx0', 'ldx1', 'ldx2', 'ldx3', 'ldx4', 'ldx5', 'ldx6', 'ldx7', 'ldp', 'ldc', 'ldm', 'st', 'spill', 'rld']
        sems = {k: es.enter_context(nc.semaphore(k)) for k in sem_names}
        block = es.enter_context(nc.Block())

        o_XR, o_HR, o_HPR, o_ACTR, o_WS, o_MISC = 0, 32768, 49152, 65536, 76800, 93184

        def bf(o, n):
            return A[:, o:o + n]

        def fp(o, n):
            return A[:, o:o + 2 * n].bitcast(F32)

        X = fp(o_XR, 16384).rearrange("p (c t) -> p c t", c=16)
        H = bf(o_HR, 16384).rearrange("p (c t) -> p c t", c=16)
        HP = bf(o_HPR, 16384).rearrange("p (c t) -> p c t", c=16)
        MERGED = HP
        ACTB = bf(o_ACTR, GJ * 1024).rearrange("p (c t) -> p c t", c=GJ)
        WSL = [bf(o_WS + i * SLOT, SLOT) for i in range(NS)]
        m = o_MISC

        def al(n):
            return (n + 127) // 128 * 128

        CST = bf(m, 384); m += al(384)
        ONES, IDENT, MASKNEG = CST[:, 0:128], CST[:, 128:256], CST[:, 256:384]
        ONESF = bf(m, 128); m += al(128)
        PRM = fp(m, NP); m += al(2 * NP)
        RS = [fp(m + i * 1024, 512) for i in range(2)]; m += 2048
        SQ = [bf(m + i * 512, 512) for i in range(4)]; m += 2048
        o_SG = m
        SG = [fp(m + i * 1024, 512) for i in range(2)]; m += 2048
        SC = fp(m, 16); m += al(32)
        LAMP = fp(m, 256); m += al(512)
        assert m <= 103 * 1024 and o_SG + 4096 <= 103 * 1024
        YA = bf(o_XR, 16384).rearrange("p (c t) -> p c t", c=16)
        YC = bf(o_XR + 16384, 16384).rearrange("p (c t) -> p c t", c=16)
        o = o_XR + 16384
        Qb = bf(o, 2048).rearrange("p (c t) -> p c t", c=2); o += 2048
        KT = bf(o, 4096).rearrange("p (c t) -> p c t", c=2); o += 4096
        Vb = bf(o, 4096).rearrange("p (c t) -> p c t", c=16); o += 4096
        o = o_ACTR
        OD = fp(o, 2048).rearrange("p (c t) -> p c t", c=2); o += 4096
        T1 = fp(o, 1024).rearrange("p (c t) -> p c t", c=2); o += 2048
        R1 = fp(o, 512); o += 1024
        R2 = fp(o, 512); o += 1024
        PT = [bf(o + i * 512, 512) for i in range(4)]; o += 2048
        o = o_ACTR
        Ub = fp(o, 1024); o += 2048
        CUH = fp(o, 1028); o += 2176
        Ab = fp(o, 1024); o += 2048
        HAL = fp(o, 4); o += 128
        o = o_ACTR
        SGA = fp(o, 1024); o += 2048
        SGC = fp(o, 1024); o += 2048
        M1 = fp(o, 1024); o += 2048
        M2 = fp(o, 1024); o += 2048
        MT = fp(o_HR, 4096).rearrange("p (c t) -> p c t", c=16)
        MH = bf(o_SG, 4096).rearrange("p (c t) -> p c t", c=16)
        QX = bf(o_HPR + 12288, 4096).rearrange("p (c t) -> p c t", c=4)
        o = o_ACTR
        KX = bf(o, 1024).rearrange("p (c t) -> p c t", c=4); o += 1024
        VX = bf(o, 1024).rearrange("p (c t) -> p c t", c=2); o += 1024
        OX = bf(o, 4096).rearrange("p (c t) -> p c t", c=4); o += 4096
        PTX = [bf(o + i * 512, 512) for i in range(4)]; o += 2048
        RX = fp(o, 512); o += 1024

        wr_tok = {}
        rd_tok = {}
        for b_ in range(8):
            wr_tok[b_] = None
            rd_tok[b_] = []
            for h_ in range(2):
                wr_tok[(b_, h_)] = None
                rd_tok[(b_, h_)] = []

        def MM(bank, out, lhsT, rhs, start, stop, waits=(), sig=None, sgc=False):
            ws = list(waits)
            if start:
                if isinstance(bank, tuple):
                    ws.append(list(rd_tok[bank]))
                    ws.append(list(rd_tok[bank[0]]))
                    rd_tok[bank] = []
                else:
                    for k_ in (bank, (bank, 0), (bank, 1)):
                        ws.append(list(rd_tok[k_]))
                        rd_tok[k_] = []
            if sig is None:
                sig = stop
            if sgc:
                fn = lambda e, o=out, l=lhsT, r=rhs, s=start, p=stop: e.matmul(o, l, r, start=s, stop=p, skip_group_check=True)
            else:
                fn = lambda e, o=out, l=lhsT, r=rhs, s=start, p=stop: e.matmul(o, l, r, start=s, stop=p)
            tok = S.op('pe', fn, ws, sig)
            if stop:
                wr_tok[bank] = tok
            return tok

        def EV(eng, fn, banks=(), waits=()):
            ws = list(waits) + [wr_tok[b] for b in banks]
            tok = S.op(eng, fn, ws, sig=True)
            for b in banks:
                rd_tok[b].append(tok)
            return tok

        def act_copy(out, in_):
            return lambda e, o=out, i=in_: e.activation(out=o, in_=i, func=AF.Copy)

        def act_fn(out, in_, func, **kw):
            return lambda e, o=out, i=in_, f=func, k=kw: e.activation(out=o, in_=i, func=f, **k)

        def dve_copy(out, in_):
            return lambda e, o=out, i=in_: e.tensor_copy(out=o, in_=i)

        def dve_tt(out, a, b, op):
            return lambda e, o=out, x=a, y=b, p=op: e.tensor_tensor(out=o, in0=x, in1=y, op=p)

        def dve_ts(out, a, s1, s2, op0, op1=None):
            if op1 is None:
                return lambda e, o=out, x=a, s=s1, p=op0: e.tensor_scalar(out=o, in0=x, scalar1=s, scalar2=None, op0=p)
            return lambda e, o=out, x=a, s=s1, s_2=s2, p=op0, q=op1: e.tensor_scalar(out=o, in0=x, scalar1=s, scalar2=s_2, op0=p, op1=q)

        def dve_stt(out, a, s, b, op0, op1):
            return lambda e, o=out, x=a, sc=s, y=b, p=op0, q=op1: e.scalar_tensor_tensor(out=o, in0=x, scalar=sc, in1=y, op0=p, op1=q)

        def dve_recip(out, in_):
            return lambda e, o=out, i=in_: e.reciprocal(out=o, in_=i)

        def barrier(engs=('pe', 'act', 'dve')):
            if not _os.environ.get("K_BARRIER"):
                return
            toks = [S.last.get(e) for e in ('pe', 'act', 'dve')]
            for e in engs:
                S.op(e, None, toks)

        wstate = {'i': 0, 'rel': {}}

        def fetch(key):
            off, size = OFF[key]
            i = wstate['i']
            wstate['i'] += 1
            slot = i % NS
            waits = []
            if i >= NS:
                waits.append(wstate['rel'][i - NS])
            elif i >= 1 and 'x0' in wstate:
                waits.append(list(wstate['x0'] if i == 1 else wstate['xall']))
            tok = S.op('pool', lambda e, s=slot, o=off, n=size: e.dma_start(out=WSL[s][:, 0:n], in_=w_d[:, o:o + n]),
                       waits, dma_sem=f'w{slot}', wr=[WSL[slot][:, 0:size]])
            return i, WSL[slot], tok

        def release(i, tok):
            wstate['rel'][i] = tok

        def tsl(t):
            return slice(t * T, (t + 1) * T)

        t_prm = S.op('sp', lambda e: e.dma_start(out=PRM, in_=prm_d), dma_sem='ldp', wr=[PRM])
        t_cst = S.op('pool', lambda e: e.dma_start(out=CST, in_=cst_d), dma_sem='ldc', wr=[CST])
        tk = S.op('dve', dve_ts(ONESF, ONES, PRM[:, P_FLAG:P_FLAG + 1], None, ALU.mult), [t_prm, t_cst], sig=True)
        LAMV = PRM[:, P_LAM:P_LAM + 512].rearrange("p (c t) -> p c t", c=4)
        LP = LAMP.rearrange("p (c t) -> p c t", c=2)
        tk = S.op('dve', dve_tt(LP[:, 0, :], LAMV[:, 0, :], LAMV[:, 1, :], ALU.mult), [t_prm], sig=True)
        tk = S.op('dve', dve_tt(LP[:, 1, :], LAMV[:, 2, :], LAMV[:, 3, :], ALU.mult), [tk], sig=True)
        tk = S.op('dve', lambda e: e.reduce_sum(out=SC[:, 0:1], in_=LP[:, 0, :], axis=mybir.AxisListType.X), [tk], sig=True, rd=[LAMP], wr=[SC])
        tk = S.op('dve', lambda e: e.reduce_sum(out=SC[:, 1:2], in_=LP[:, 1, :], axis=mybir.AxisListType.X), [tk], sig=True, rd=[LAMP], wr=[SC])
        tk = S.op('act', act_fn(SC[:, 2:4], SC[:, 0:2], AF.Exp), [tk], sig=True)
        tk = S.op('dve', dve_tt(SC[:, 4:5], SC[:, 2:3], SC[:, 3:4], ALU.subtract), [tk], sig=True)
        tk = S.op('dve', dve_ts(SC[:, 5:6], SC[:, 4:5], 0.2, -1.0, ALU.add, ALU.mult), [tk], sig=True)
        NEGLAM = SC[:, 5:6]
        t_setup = tk

        def rmsnorm(src, nchunk, ntile, tw, gcol, dst, eps, ndim, waits, stat_banks=(6, 7), post_scale=None, cwaits=None, ctoks=None):
            first = True
            last = None
            tile_toks = []
            for t in range(ntile):
                sl = slice(t * tw, (t + 1) * tw)
                bank = stat_banks[t % 2]
                for c in range(nchunk):
                    w = [waits] if first else []
                    first = False
                    if cwaits is not None and t == 0:
                        w.append(cwaits[c])
                    k = c % 4
                    tsq = S.op('act', act_fn(SQ[k][:, 0:tw], src[:, c, sl], AF.Square), w + [sq_rd[k]], sig=True)
                    sq_rd[k] = MM(bank, PS[bank][:, 0:tw], ONES, SQ[k][:, 0:tw], c == 0, c == nchunk - 1, [tsq], sig=True)
                r = RS[t % 2][:, 0:tw]
                t1 = EV('dve', dve_ts(r, PS[bank][:, 0:tw], 1.0 / ndim, eps, ALU.mult, ALU.add), [bank], [rs_rd[t % 2]])
                t2 = S.op('act', act_fn(r, r, AF.Sqrt), [t1], sig=True)
                t3 = S.op('dve', dve_recip(r, r), [t2], sig=True)
                if post_scale is not None:
                    t3 = S.op('dve', dve_ts(r, r, post_scale, None, ALU.mult), [t3], sig=True)
                for c in range(nchunk):
                    last = S.op('dve', dve_stt(dst[:, c, sl], src[:, c, sl], PRM[:, gcol + c:gcol + c + 1], r, ALU.mult, ALU.mult),
                                [t3], sig=True)
                    if ctoks is not None:
                        ctoks[c] = last
                rs_rd[t % 2] = last
                tile_toks.append(last)
            return tile_toks

        sq_rd = [None] * 4
        rs_rd = [None] * 2

        def ffn(f, h_ready):
            sg_rd = [None, None]
            last_down_pe = None
            x_tok = None
            for g in range(NG):
                nj = min(GJ, NJ - GJ * g)
                act_tok = None
                for jj in range(nj):
                    j = g * GJ + jj
                    bi, blk, ltok = fetch(('gu', f, j))
                    bv = blk.rearrange("p (a k m) -> p a k m", a=2, k=16)
                    tg = [None, None]
                    tu = [None, None]
                    if j == 0:
                        for t in range(2):
                            for kc in range(16):
                                tg[t] = MM(t, PS[t], bv[:, 0, kc, :], H[:, kc, tsl(t)], kc == 0, kc == 15,
                                           [ltok, h_ready[t]] if kc == 0 else [])
                            for kc in range(16):
                                tu[t] = MM(2 + t, PS[2 + t], bv[:, 1, kc, :], H[:, kc, tsl(t)], kc == 0, kc == 15)
                    else:
                        for kc in range(16):
                            for t in range(2):
                                tg[t] = MM(t, PS[t], bv[:, 0, kc, :], H[:, kc, tsl(t)], kc == 0, kc == 15,
                                           [ltok] if kc == 0 else [])
                        for kc in range(16):
                            for t in range(2):
                                tu[t] = MM(2 + t, PS[2 + t], bv[:, 1, kc, :], H[:, kc, tsl(t)], kc == 0, kc == 15)
                    release(bi, tu[1])
                    for t in range(2):
                        ts_ = EV('act', act_fn(SG[t], PS[t], AF.Silu), [t], [sg_rd[t]])
                        act_tok = EV('dve', dve_tt(ACTB[:, jj, tsl(t)], PS[2 + t], SG[t], ALU.mult), [2 + t],
                                     [ts_, last_down_pe])
                        sg_rd[t] = act_tok
                for c in range(16):
                    bi, blk, ltok = fetch(('dn', f, g, c))
                    bv = blk[:, 0:nj * 128].rearrange("p (j m) -> p j m", j=nj)
                    tks = [None, None]
                    for jj in range(nj):
                        for t in range(2):
                            bank = 4 + 2 * (c % 2) + t
                            tks[t] = MM(bank, PS[bank], bv[:, jj, :], ACTB[:, jj, tsl(t)], jj == 0, jj == nj - 1,
                                        [ltok] if jj == 0 else [])
                    release(bi, tks[1])
                    last_down_pe = tks[1]
                    for t in range(2):
                        bank = 4 + 2 * (c % 2) + t
                        x_tok = EV('dve', dve_stt(X[:, c, tsl(t)], PS[bank], 0.5, X[:, c, tsl(t)], ALU.mult, ALU.add), [bank])
            return x_tok

        def load_x(src_d):
            i = 0
            toks = []
            for t in range(2):
                for g4 in range(4):
                    cs = slice(4 * g4, 4 * g4 + 4)
                    toks.append(S.op('sp', lambda e, cs=cs, t=t: e.dma_start(out=X[:, cs, tsl(t)], in_=src_d[cs, :, tsl(t)].rearrange("c p t -> p c t")),
                                     [], dma_sem=f'ldx{i}', wr=[X[:, cs, tsl(t)]]))
                    i += 1
            return toks

        wstate['xall'] = load_x(xp_d)
        wstate['x0'] = wstate['xall'][0:4]
        th = rmsnorm(X, 16, 2, T, P_FFN1, H, 1e-6, D, [t_setup])
        tx1 = ffn(1, th)
        barrier()
        thp = rmsnorm(X, 16, 2, T, P_MIX, HP, 1e-6, D, [])
        load_x(xo_d)
        th = rmsnorm(X, 16, 2, T, P_FFN1, H, 1e-6, D, [])
        tx1 = ffn(1, th)
        barrier()
        th = rmsnorm(X, 16, 2, T, P_MIX, H, 1e-6, D, [])
        barrier()
        t_spill = []
        for i_, c0_ in enumerate((8, 10, 12, 14, 0, 2, 4, 6)):
            cs_ = slice(c0_, c0_ + 2)
            t_spill.append(S.op('sp', lambda e, cs=cs_: e.dma_start(out=xs_d[cs].rearrange("c p t -> p c t"), in_=X[:, cs, :]),
                                [], dma_sem=f'ldx{i_}', rd=[X[:, cs_, :]]))

        TA = 256
        PT8 = [PT[i // 2][:, (i % 2) * 256:(i % 2 + 1) * 256] for i in range(8)]

        def head_proj_q(hd):
            bi, blk, ltok = fetch(('q', hd))
            bv = blk.rearrange("p (k m) -> p k m", k=16)
            if hd == 0:
                tk = None
                for t in range(2):
                    for comp in range(2):
                        bank = 2 * comp + t
                        for kc in range(16):
                            tk = MM(bank, PS[bank], bv[:, kc, comp * 128:(comp + 1) * 128], H[:, kc, tsl(t)],
                                    kc == 0, kc == 15, [ltok] if kc == 0 else [])
                        EV('act', act_copy(Qb[:, comp, tsl(t)], PS[bank]), [bank])
                release(bi, tk)
                return
            for comp in range(2):
                banks = (0, 1) if comp == 0 else (2, 3)
                tk_ = [None, None]
                for kc in range(16):
                    for t in range(2):
                        tk_[t] = MM(banks[t], PS[banks[t]], bv[:, kc, comp * 128:(comp + 1) * 128], H[:, kc, tsl(t)],
                                    kc == 0, kc == 15, [ltok] if kc == 0 else [])
                if comp == 1:
                    release(bi, tk_[1])
                for t in range(2):
                    EV('act', act_copy(Qb[:, comp, tsl(t)], PS[banks[t]]), [banks[t]])

        def head_proj_kv(hd):
            bi, blk, ltok = fetch(('k', hd))
            bv = blk.rearrange("p (k m) -> p k m", k=16)
            for comp in range(2):
                banks = (0, 1, 5, 6) if comp == 0 else (2, 3, 4, 7)
                tk_ = [None] * 4
                for kc in range(16):
                    for tt in range(4):
                        src = HP if tt < 2 else H
                        tk_[tt] = MM(banks[tt], PS[banks[tt]], bv[:, kc, comp * 128:(comp + 1) * 128], src[:, kc, tsl(tt % 2)],
                                     kc == 0, kc == 15, [ltok] if kc == 0 else [])
                if comp == 1:
                    release(bi, tk_[3])
                for tt in range(4):
                    eng = 'dve' if tt % 2 == 0 else 'act'
                    fn = dve_copy(KT[:, comp, tsl(tt)], PS[banks[tt]]) if eng == 'dve' else act_copy(KT[:, comp, tsl(tt)], PS[banks[tt]])
                    EV(eng, fn, [banks[tt]])
            bi, blk, ltok = fetch(('v', hd))
            bv = blk.rearrange("p (k m) -> p k m", k=16)
            tk_ = None
            for tc in range(16):
                src = HP if tc < 8 else H
                tcl = tc % 8
                bank = 4 + (tc % 4)
                for kc in range(16):
                    tk_ = MM(bank, PS[bank][:, 0:256], src[:, kc, tcl * 128:(tcl + 1) * 128], bv[:, kc, :],
                             kc == 0, kc == 15, [ltok] if kc == 0 else [])
                if tc < 8:
                    EV('dve', dve_ts(Vb[:, tc, :], PS[bank][:, 0:256], PRM[:, P_FLAG:P_FLAG + 1], None, ALU.mult), [bank])
                else:
                    EV('act', act_copy(Vb[:, tc, :], PS[bank][:, 0:256]), [bank])
            release(bi, tk_)
            return [S.last['act'], S.last['dve']]

        def head_attn(hd, q_ready):
            steps = []
            for qt in range(4):
                for comp in range(2):
                    lst = [(kc, 0, False, True) for kc in range(8)]
                    for oc in range(2 * qt + 2):
                        if oc < 2 * qt:
                            lst.append((8 + oc, 0, False, False))
                        else:
                            lst.append((8 + oc, (oc - 2 * qt) * 128, True, False))
                    for i, (kcg, col0, diag, prev) in enumerate(lst):
                        steps.append(dict(qt=qt, comp=comp, kcg=kcg, col0=col0, diag=diag, prev=prev,
                                          first=(i == 0), last=(i == len(lst) - 1)))
            n = len(steps)
            exp_tok = [None] * n
            pv_tok = [None] * n
            PD = 3

            def sview(s, a, b_):
                bank = s % 4
                return bank, PS[bank][:, a:b_]

            def qk(s):
                st = steps[s]
                c0 = st['col0']
                key, out = sview(s, c0, 256)
                w = []
                tk_ = MM(key, out, KT[:, st['comp'], st['kcg'] * 128:(st['kcg'] + 1) * 128],
                         Qb[:, st['comp'], st['qt'] * TA + c0:(st['qt'] + 1) * TA], True, not st['diag'], w)
                if st['diag']:
                    key, out = sview(s, c0, c0 + 128)
                    tk_ = MM(key, out, IDENT, MASKNEG, False, True)
                return tk_

            def ex(s):
                st = steps[s]
                c0 = st['col0']
                key, src = sview(s, c0, 256)
                w = [pv_tok[s - 8]] if s >= 8 else []
                exp_tok[s] = EV('act', act_fn(PT8[s % 8][:, c0:256], src, AF.Exp, scale=SCALE), [key], w)

            def pv(s):
                st = steps[s]
                c0 = st['col0']
                comp = st['comp']
                qt = st['qt']
                ob = 4 if comp == 0 else 6
                lb = 5 if comp == 0 else 7
                p = PT8[s % 8][:, c0:256]
                for dvc in range(2):
                    MM(ob, PS[ob][:, dvc * 256 + c0:(dvc + 1) * 256], Vb[:, st['kcg'], dvc * 128:(dvc + 1) * 128], p,
                       st['first'] and dvc == 0, st['last'], [exp_tok[s]] if dvc == 0 else [], sgc=True)
                pv_tok[s] = MM(lb, PS[lb][:, c0:256], ONESF if st['prev'] else ONES, p, st['first'], st['last'], sig=True)
                if st['last']:
                    qs = slice(qt * TA, (qt + 1) * TA)
                    if comp == 0:
                        a = EV('dve', dve_recip(R1[:, 0:256], PS[5][:, 0:256]), [5])
                        for dvc in range(2):
                            EV('dve', dve_tt(T1[:, dvc, 0:256], PS[4][:, dvc * 256:(dvc + 1) * 256], R1[:, 0:256], ALU.mult), [4], [a])
                    else:
                        a = EV('dve', dve_recip(R2[:, 0:256], PS[7][:, 0:256]), [7])
                        a = S.op('dve', dve_ts(R2[:, 0:256], R2[:, 0:256], NEGLAM, None, ALU.mult), [a], sig=True)
                        for dvc in range(2):
                            b = EV('dve', dve_tt(OD[:, dvc, qs], PS[6][:, dvc * 256:(dvc + 1) * 256], R2[:, 0:256], ALU.mult), [6], [a])
                            S.op('dve', dve_tt(OD[:, dvc, qs], OD[:, dvc, qs], T1[:, dvc, 0:256], ALU.add), [b], sig=True)

            for s in range(min(PD, n)):
                qk(s)
                ex(s)
            for s in range(n):
                if s + PD < n:
                    qk(s + PD)
                    ex(s + PD)
                pv(s)
            return S.last['dve']

        def head_subln_sq(od_tok):
            toks = {}
            for t in range(2):
                for dvc in range(2):
                    k = 2 * t + dvc
                    toks[(t, dvc)] = S.op('act', act_fn(SQ[k], OD[:, dvc, tsl(t)], AF.Square), [od_tok, sq_rd[k]], sig=True)
            return toks

        def head_subln(hd, sqt):
            for t in range(2):
                bank = 4 if t == 0 else 7
                for dvc in range(2):
                    k = 2 * t + dvc
                    sq_rd[k] = MM(bank, PS[bank], ONES, SQ[k], dvc == 0, dvc == 1, [sqt[(t, dvc)]], sig=True)
                r = RS[t]
                t1 = EV('dve', dve_ts(r, PS[bank], 1.0 / 256, 1e-5, ALU.mult, ALU.add), [bank], [rs_rd[t]])
                t2 = S.op('act', act_fn(r, r, AF.Sqrt), [t1], sig=True)
                t3 = S.op('dve', dve_recip(r, r), [t2], sig=True)
                t3 = S.op('dve', dve_ts(r, r, 0.8, None, ALU.mult), [t3], sig=True)
                for dvc in range(2):
                    rs_rd[t] = S.op('dve', dve_stt(YA[:, 2 * hd + dvc, tsl(t)], OD[:, dvc, tsl(t)],
                                                   PRM[:, P_SUB + dvc:P_SUB + dvc + 1], r, ALU.mult, ALU.mult), [t3], sig=True)

        sqt = None
        for hd in range(8):
            head_proj_q(hd)
            if hd > 0:
                head_subln(hd - 1, sqt)
            qr = head_proj_kv(hd)
            od_tok = head_attn(hd, qr)
            sqt = head_subln_sq(od_tok)
        head_subln(7, sqt)
        barrier(engs=('act', 'dve'))

        CU = CUH[:, 2:1026]
        for c in range(16):
            tkc = {}
            for nm, b0, hb in (('cC', 0, 6), ('cU', 2, 7), ('cB', 4, None)):
                bi, blk, ltok = fetch((nm, c))
                bv = blk[:, 0:2048].rearrange("p (k m) -> p k m", k=16)
                tk_ = [None, None]
                for kc in range(16):
                    for t in range(2):
                        tk_[t] = MM(b0 + t, PS[b0 + t], bv[:, kc, :], H[:, kc, tsl(t)], kc == 0, kc == 15,
                                    [ltok] if kc == 0 else [])
                if hb is not None:
                    for kc in range(16):
                        tk_[1] = MM(hb, PS[hb][:, 0:2], bv[:, kc, :], HP[:, kc, 1022:1024], kc == 0, kc == 15)
                release(bi, tk_[1])
            tu_ = None
            for t in range(2):
                tu_ = EV('act', act_copy(Ub[:, tsl(t)], PS[2 + t]), [2 + t], [conv_rd] if c > 0 and t == 0 else [])
            th_ = EV('act', act_copy(HAL[:, 0:2], PS[7][:, 0:2]), [7])
            a = None
            for t in range(2):
                a = EV('dve', dve_tt(CU[:, tsl(t)], PS[t], Ub[:, tsl(t)], ALU.mult), [t], [tu_])
            a = EV('dve', dve_tt(CUH[:, 0:2], PS[6][:, 0:2], HAL[:, 0:2], ALU.mult), [6], [th_, a])
            a = S.op('dve', dve_ts(CUH[:, 0:2], CUH[:, 0:2], PRM[:, P_FLAG:P_FLAG + 1], None, ALU.mult), [a], sig=True)
            w0 = PRM[:, P_CW + c:P_CW + c + 1]
            w1 = PRM[:, P_CW + 16 + c:P_CW + 16 + c + 1]
            w2 = PRM[:, P_CW + 32 + c:P_CW + 32 + c + 1]
            a = S.op('dve', dve_ts(Ab, CUH[:, 0:1024], w0, None, ALU.mult), [a], sig=True)
            a = S.op('dve', dve_stt(Ab, CUH[:, 1:1025], w1, Ab, ALU.mult, ALU.add), [a], sig=True)
            a = S.op('dve', dve_stt(Ab, CUH[:, 2:1026], w2, Ab, ALU.mult, ALU.add), [a], sig=True)
            for t in range(2):
                a = EV('dve', dve_tt(YC[:, c, tsl(t)], PS[4 + t], Ab[:, tsl(t)], ALU.mult), [4 + t], [a])
            conv_rd = a
        barrier()

        mrd = None
        for c in range(16):
            specs = (('ga', H, 0), ('gc', H, 2), ('ao', YA, 4), ('co', YC, 6))
            for nm, src, b0 in specs:
                bi, blk, ltok = fetch((nm, c))
                bv = blk[:, 0:2048].rearrange("p (k m) -> p k m", k=16)
                tk_ = [None, None]
                for kc in range(16):
                    for t in range(2):
                        tk_[t] = MM(b0 + t, PS[b0 + t], bv[:, kc, :], src[:, kc, tsl(t)], kc == 0, kc == 15,
                                    [ltok] if kc == 0 else [])
                release(bi, tk_[1])
            s1 = s2 = None
            for t in range(2):
                s1 = EV('act', act_fn(SGA[:, tsl(t)], PS[0 + t], AF.Sigmoid, bias=PRM[:, P_BGA + c:P_BGA + c + 1]), [0 + t],
                        [mrd] if t == 0 else [])
                s2 = EV('act', act_fn(SGC[:, tsl(t)], PS[2 + t], AF.Sigmoid, bias=PRM[:, P_BGC + c:P_BGC + c + 1]), [2 + t])
            for t in range(2):
                a = EV('dve', dve_tt(M1[:, tsl(t)], PS[4 + t], SGA[:, tsl(t)], ALU.mult), [4 + t], [s1, s2])
                b = EV('dve', dve_tt(M2[:, tsl(t)], PS[6 + t], SGC[:, tsl(t)], ALU.mult), [6 + t], [a])
                mrd = S.op('dve', dve_tt(MERGED[:, c, tsl(t)], M1[:, tsl(t)], M2[:, tsl(t)], ALU.add), [b], sig=True)
        barrier()
        t_rld = [S.op('sp', lambda e, q=q: e.dma_start(out=X[:, 4 * q:4 * q + 4, :], in_=xs_d[4 * q:4 * q + 4].rearrange("c p t -> p c t")),
                      [S.last['pe'], S.last['dve'], S.last['act'], t_spill], dma_sem=f'ldx{q}', wr=[X[:, 4 * q:4 * q + 4, :]]) for q in range(4)]
        t_mem = S.op('sp', lambda e: e.dma_start(out=MT, in_=mem_d.rearrange("c p t -> p c t")), [], dma_sem='ldm', wr=[MT])
        xt = None
        for c in range(16):
            if c == 8:
                tmh = rmsnorm(MT, 16, 1, 256, P_MEM, MH, 1e-6, D, [t_mem])
            bi, blk, ltok = fetch(('mo', c))
            bv = blk[:, 0:2048].rearrange("p (k m) -> p k m", k=16)
            tk_ = [None, None]
            b0 = 2 * (c % 4)
            for kc in range(16):
                for t in range(2):
                    tk_[t] = MM(b0 + t, PS[b0 + t], bv[:, kc, :], MERGED[:, kc, tsl(t)], kc == 0, kc == 15,
                                [ltok] if kc == 0 else [])
            release(bi, tk_[1])
            for t in range(2):
                xt = EV('dve', dve_tt(X[:, c, tsl(t)], PS[b0 + t], X[:, c, tsl(t)], ALU.add), [b0 + t], [t_rld[c // 4]])
        barrier()

        th = rmsnorm(X, 16, 2, T, P_XA, H, 1e-6, D, [])
        for hd in range(4):
            bi, blk, ltok = fetch(('xk', hd))
            bv = blk[:, 0:2048].rearrange("p (k m) -> p k m", k=16)
            tk_ = None
            bank = hd % 2
            for kc in range(16):
                tk_ = MM(bank, PS[bank][:, 0:256], bv[:, kc, :], MH[:, kc, :], kc == 0, kc == 15, [ltok] if kc == 0 else [])
            release(bi, tk_)
            EV('act', act_copy(KX[:, hd, :], PS[bank][:, 0:256]), [bank])
        for i in range(2):
            bi, blk, ltok = fetch(('xv', i))
            bv = blk.rearrange("p (k m) -> p k m", k=16)
            tk_ = None
            for mc in range(2):
                bank = 2 + mc
                for kc in range(16):
                    tk_ = MM(bank, PS[bank][:, 0:256], MH[:, kc, mc * 128:(mc + 1) * 128], bv[:, kc, :], kc == 0, kc == 15,
                             [ltok] if kc == 0 else [])
                EV('dve', dve_copy(VX[:, mc, i * 256:(i + 1) * 256], PS[bank][:, 0:256]), [bank])
            release(bi, tk_)
        qb = [fetch(('xq', hd)) for hd in range(4)]
        for t in range(2):
            for hd in range(4):
                bi, blk, ltok = qb[hd]
                bv = blk[:, 0:2048].rearrange("p (k m) -> p k m", k=16)
                bank = 4 + hd
                tk = None
                for kc in range(16):
                    tk = MM(bank, PS[bank], bv[:, kc, :], H[:, kc, tsl(t)], kc == 0, kc == 15, [ltok] if kc == 0 else [])
                if t == 1:
                    release(bi, tk)
                EV('act' if hd % 2 == 0 else 'dve',
                   act_copy(QX[:, hd, tsl(t)], PS[bank]) if hd % 2 == 0 else dve_copy(QX[:, hd, tsl(t)], PS[bank]), [bank])
        barrier()
        xsteps = [(hd, t, mc) for hd in range(4) for t in range(2) for mc in range(2)]
        nx = len(xsteps)
        x_exp = [None] * nx
        x_pv = [None] * nx

        def x_qk(si):
            hd, t, mc = xsteps[si]
            sb = si % 4
            MM(sb, PS[sb], KX[:, hd, mc * 128:(mc + 1) * 128], QX[:, hd, tsl(t)], True, True)
            x_exp[si] = EV('act', act_fn(PTX[si % 4], PS[sb], AF.Exp, scale=SCALE), [sb],
                           [x_pv[si - 4]] if si >= 4 else [])

        def x_pvf(si):
            hd, t, mc = xsteps[si]
            ob = 4 + ((si // 2) % 2) * 2
            p = PTX[si % 4]
            MM(ob, PS[ob], VX[:, mc, hd * 128:(hd + 1) * 128], p, mc == 0, mc == 1, [x_exp[si]])
            x_pv[si] = MM(ob + 1, PS[ob + 1], ONES, p, mc == 0, mc == 1, sig=True)
            if mc == 1:
                a = EV('dve', dve_recip(RX, PS[ob + 1]), [ob + 1])
                EV('dve', dve_tt(OX[:, hd, tsl(t)], PS[ob], RX, ALU.mult), [ob], [a])

        XPD = 2
        for si in range(min(XPD, nx)):
            x_qk(si)
        for si in range(nx):
            if si + XPD < nx:
                x_qk(si + XPD)
            x_pvf(si)
        blks = [fetch(('xo', i)) for i in range(2)]
        for c in range(16):
            bi, blk, ltok = blks[c // 8]
            bv = blk.rearrange("p (k m) -> p k m", k=4)
            b0 = 2 * (c % 4)
            tk_ = [None, None]
            for kc in range(4):
                for t in range(2):
                    tk_[t] = MM(b0 + t, PS[b0 + t], bv[:, kc, (c % 8) * 128:(c % 8 + 1) * 128], OX[:, kc, tsl(t)], kc == 0, kc == 3,
                                [ltok] if kc == 0 else [])
            if c % 8 == 7:
                release(bi, tk_[1])
            for t in range(2):
                xt = EV('dve', dve_tt(X[:, c, tsl(t)], PS[b0 + t], X[:, c, tsl(t)], ALU.add), [b0 + t])
        barrier()

        th = rmsnorm(X, 16, 2, T, P_FFN2, H, 1e-6, D, [])
        tx2 = ffn(2, th)
        barrier()
        tf = rmsnorm(X, 16, 2, T, P_FIN, X, 1e-6, D, [])
        t_out = None
        for t in range(2):
            for g4 in range(4):
                cs = slice(4 * g4, 4 * g4 + 4)
                t_out = S.op('sp', lambda e, cs=cs, t=t: e.dma_start(out=out_d[cs, :, tsl(t)].rearrange("c p t -> p c t"), in_=X[:, cs, tsl(t)]),
                             [], dma_sem='st', rd=[X[:, cs, tsl(t)]])
        S.op('sp', None, [t_out])

        def replay(name, eng):
            seen = {}
            for fn, waits, tok, is_dma in S.ops[name]:
                for (k, v) in waits:
                    if seen.get(k, 0) >= v:
                        continue
                    eng.wait_ge(sems[k], v)
                    seen[k] = v
                if fn is None:
                    continue
                ins = fn(eng)
                if tok is not None:
                    ins.then_inc(sems[tok[0]], 16 if is_dma else 1)

        @block.tensor
        def _(e):
            replay('pe', e)

        @block.scalar
        def _(e):
            replay('act', e)

        @block.vector
        def _(e):
            replay('dve', e)

        @block.gpsimd
        def _(e):
            replay('pool', e)

        @block.sync
        def _(e):
            replay('sp', e)
    return nc


def _blk_kn(W, n0, n1):
    K = W.shape[0]
    kc = K // 128
    n = n1 - n0
    return np.ascontiguousarray(W[:, n0:n1].reshape(kc, 128, n).transpose(1, 0, 2)).reshape(128, kc * n)


def _host_weights(inp):
    OFF, TOTAL = plan()
    wb = np.empty((128, TOTAL), np.float32)

    def put(key, arr):
        o, s = OFF[key]
        assert arr.shape == (128, s), (key, arr.shape, s)
        wb[:, o:o + s] = arr

    for f in (1, 2):
        Wg, Wu, Wd = inp[f'ffn{f}_w_gate'][0], inp[f'ffn{f}_w_up'][0], inp[f'ffn{f}_w_down'][0]
        for j in range(NJ):
            put(('gu', f, j), np.concatenate([_blk_kn(Wg, j * 128, (j + 1) * 128), _blk_kn(Wu, j * 128, (j + 1) * 128)], axis=1))
        for g in range(NG):
            nj = min(GJ, NJ - GJ * g)
            rows = Wd[g * GJ * 128:(g * GJ + nj) * 128]
            for c in range(16):
                put(('dn', f, g, c), _blk_kn(rows, c * 128, (c + 1) * 128))
    Wmi = inp['w_mix_in'][0]
    for hd in range(8):
        put(('q', hd), _blk_kn(Wmi, hd * 256, hd * 256 + 256))
        put(('k', hd), _blk_kn(Wmi, 2048 + hd * 256, 2048 + hd * 256 + 256))
        put(('v', hd), _blk_kn(Wmi, 4096 + hd * 256, 4096 + hd * 256 + 256))
    for c in range(16):
        for nm, base in (('cB', 6144), ('cC', 8192), ('cU', 10240), ('ga', 12288), ('gc', 14336)):
            put((nm, c), _blk_kn(Wmi, base + c * 128, base + (c + 1) * 128))
        put(('ao', c), _blk_kn(inp['w_attn_out'][0], c * 128, (c + 1) * 128))
        put(('co', c), _blk_kn(inp['w_conv_out'][0], c * 128, (c + 1) * 128))
        put(('mo', c), _blk_kn(inp['w_mix_out'][0], c * 128, (c + 1) * 128))
    for hd in range(4):
        put(('xq', hd), _blk_kn(inp['w_xq'][0], hd * 128, (hd + 1) * 128))
        put(('xk', hd), _blk_kn(inp['w_xkv'][0], hd * 128, (hd + 1) * 128))
    for i in range(2):
        put(('xv', i), _blk_kn(inp['w_xkv'][0], 512 + i * 256, 512 + (i + 1) * 256))
        put(('xo', i), _blk_kn(inp['w_xo'][0], i * 1024, (i + 1) * 1024))
    return wb


def _colmajor16(v):
    return np.ascontiguousarray(np.asarray(v, np.float32).reshape(16, 128).T)


def kernel(**inp):
    inp = {k: np.asarray(v) for k, v in inp.items()}
    x = inp['x'].astype(np.float32, copy=False)
    mem = inp['mem'].astype(np.float32, copy=False)
    wb = _host_weights(inp)
    prm = np.zeros((128, NP), np.float32)
    prm[:, P_FFN1:P_FFN1 + 16] = _colmajor16(inp['ffn1_norm'][0])
    prm[:, P_MIX:P_MIX + 16] = _colmajor16(inp['mix_norm'][0])
    prm[:, P_XA:P_XA + 16] = _colmajor16(inp['xattn_norm'][0])
    prm[:, P_MEM:P_MEM + 16] = _colmajor16(inp['mem_norm'][0])
    prm[:, P_FFN2:P_FFN2 + 16] = _colmajor16(inp['ffn2_norm'][0])
    prm[:, P_FIN:P_FIN + 16] = _colmajor16(inp['final_norm'])
    prm[:, P_BGA:P_BGA + 16] = _colmajor16(inp['b_gates'][0, 0])
    prm[:, P_BGC:P_BGC + 16] = _colmajor16(inp['b_gates'][0, 1])
    for j in range(3):
        prm[:, P_CW + 16 * j:P_CW + 16 * j + 16] = _colmajor16(inp['conv_w'][0, j])
    prm[:, P_SUB:P_SUB + 2] = np.asarray(inp['diff_subln'][0], np.float32).reshape(2, 128).T
    for i, nm in enumerate(('lambda_q1', 'lambda_k1', 'lambda_q2', 'lambda_k2')):
        prm[:, P_LAM + 128 * i:P_LAM + 128 * (i + 1)] = np.asarray(inp[nm][0], np.float32)[None, :]
    cst = np.zeros((128, 384), np.float32)
    cst[:, 0:128] = 1.0
    cst[:, 128:256] = np.eye(128, dtype=np.float32)
    kk = np.arange(128)[:, None]
    qq = np.arange(128)[None, :]
    cst[:, 256:384] = np.where(kk > qq, -30000.0, 0.0)

    def tr(a):
        return np.ascontiguousarray(a.T).reshape(16, 128, a.shape[0])

    in_maps = []
    for core in range(8):
        b, half = core // 2, core % 2
        p = prm.copy()
        p[:, P_FLAG] = float(half)
        in_maps.append({
            "x_own": tr(x[b, half * TOK:(half + 1) * TOK]),
            "x_prev": tr(x[b, 0:TOK]) if half == 1 else np.zeros((16, 128, TOK), np.float32),
            "memT": tr(mem[b]),
            "prm": p,
            "cst": cst,
            "wbig": wb,
        })
    nc = build()
    res = run_bass_kernel_spmd(nc, in_maps, core_ids=list(range(8)))
    out = np.empty((4, 2048, D), np.float32)
    for core in range(8):
        b, half = core // 2, core % 2
        o = np.asarray(res.results[core]["out"]).reshape(D, TOK)
        out[b, half * TOK:(half + 1) * TOK, :] = o.T
    return out
```
